# Optimizing a Trainium2 kernel written in Bass

```python
import math
import jax, jax.numpy as jnp
from jax import lax
import numpy as np

D_MODEL = 1024
BATCH = 2
SEQ = 8192
DEPTH = 4
DEC_BATCH = 16
DEC_SEQ = 16
PAST_LEN = 2048

CHUNK = 64
N_A = DEPTH // 2
N_B = DEPTH - N_A
N_HEADS = 16
HEAD_DIM = D_MODEL // N_HEADS
CONV_W = 3
D_FF = 2816
Q_BLOCK = 128
LN_EPS = 1e-5
DN_ALPHA = (2.0 * DEPTH) ** 0.25
DN_BETA = (8.0 * DEPTH) ** -0.25

kernel_name = "yoco_shortconv_stickbreaking_streaming_step"


def layer_norm(x, g, b):
    xf = x.astype(jnp.float32)
    mu = jnp.mean(xf, axis=-1, keepdims=True)
    var = jnp.mean(jnp.square(xf - mu), axis=-1, keepdims=True)
    y = (xf - mu) * lax.rsqrt(var + LN_EPS) * g.astype(jnp.float32) + b.astype(jnp.float32)
    return y.astype(x.dtype)


def swiglu(x, wg, wu, wd):
    return (jax.nn.silu(x @ wg) * (x @ wu)) @ wd


def short_conv_mixer(x, prev, w_in, w_conv, w_out):
    T = x.shape[1]
    gb, gc, h = jnp.split(x @ w_in, 3, axis=-1)
    u = gc * h
    up = jnp.concatenate([prev.astype(u.dtype), u], axis=1)
    conv = sum(w_conv[i] * up[:, i:i + T] for i in range(CONV_W))
    y = (gb * conv) @ w_out
    return y, up[:, -(CONV_W - 1):]


def stick_breaking(q, k, v, q_pos, k_pos):
    z = jnp.einsum('bqhd,bkhd->bhqk', q, k).astype(jnp.float32) * (HEAD_DIM ** -0.5)
    vis = k_pos[None, :] < q_pos[:, None]
    log_keep = jnp.where(vis, jax.nn.log_sigmoid(-z), 0.0)
    after = lax.cumsum(log_keep, axis=3, reverse=True) - log_keep
    w = jnp.where(vis, jnp.exp(jax.nn.log_sigmoid(z) + after), 0.0)
    o = jnp.einsum('bhqk,bkhd->bqhd', w, v.astype(jnp.float32))
    return o.astype(q.dtype)


def sb_prompt(q, k, v):
    B, T, H, Dh = q.shape
    nblk = T // Q_BLOCK
    q_blocks = q.reshape(B, nblk, Q_BLOCK, H, Dh).transpose(1, 0, 2, 3, 4)
    pos = jnp.arange(T, dtype=jnp.int32)
    pos_blocks = pos.reshape(nblk, Q_BLOCK)
    out = lax.map(lambda a: stick_breaking(a[0], k, v, a[1], pos), (q_blocks, pos_blocks))
    return out.transpose(1, 0, 2, 3, 4).reshape(B, T, H, Dh)


def trunk(x, conv_prev, cache_k, cache_v, ln_g, ln_b, w_ffn_gate, w_ffn_up, w_ffn_down,
          w_conv_in, w_conv, w_conv_out, w_kv, w_q, w_o):
    B, T, _ = x.shape
    conv_states = []
    k_new = v_new = k_all = v_all = None
    for l in range(DEPTH):
        x = layer_norm(DN_ALPHA * x + 0.5 * swiglu(x, w_ffn_gate[l, 0], w_ffn_up[l, 0], w_ffn_down[l, 0]),
                       ln_g[l, 0], ln_b[l, 0])
        if l < N_A:
            y, st = short_conv_mixer(x, conv_prev[l], w_conv_in[l], w_conv[l], w_conv_out[l])
            conv_states.append(st)
        else:
            i = l - N_A
            q = (x @ w_q[i]).reshape(B, T, N_HEADS, HEAD_DIM)
            if cache_k is None:
                o = sb_prompt(q, k_all, v_all)
            else:
                P = cache_k.shape[1]
                q_pos = P + jnp.arange(T, dtype=jnp.int32)
                k_pos = jnp.arange(P + T, dtype=jnp.int32)
                o = stick_breaking(q, k_all, v_all, q_pos, k_pos)
            y = o.reshape(B, T, D_MODEL) @ w_o[i]
        x = layer_norm(DN_ALPHA * x + y, ln_g[l, 1], ln_b[l, 1])
        x = layer_norm(DN_ALPHA * x + 0.5 * swiglu(x, w_ffn_gate[l, 1], w_ffn_up[l, 1], w_ffn_down[l, 1]),
                       ln_g[l, 2], ln_b[l, 2])
        if l == N_A - 1:
            k_new, v_new = jnp.split(x @ w_kv, 2, axis=-1)
            k_new = k_new.reshape(B, T, N_HEADS, HEAD_DIM)
            v_new = v_new.reshape(B, T, N_HEADS, HEAD_DIM)
            if cache_k is None:
                k_all, v_all = k_new, v_new
            else:
                k_all = jnp.concatenate([cache_k.astype(k_new.dtype), k_new], axis=1)
                v_all = jnp.concatenate([cache_v.astype(v_new.dtype), v_new], axis=1)
    return x, k_new, v_new, jnp.stack(conv_states, axis=0)


def setup_inputs(seed: int = 0) -> dict:
    key = jax.random.key(seed)
    ks = jax.random.split(key, 16)
    f32 = jnp.float32
    nrm = lambda k, shape, s: jax.random.normal(k, shape, f32) * s
    x_prompt = nrm(ks[0], (BATCH, SEQ, D_MODEL), 1.0)
    x_sample = nrm(ks[1], (DEC_BATCH, DEC_SEQ, D_MODEL), 1.0)
    cache_k = nrm(ks[2], (DEC_BATCH, PAST_LEN, N_HEADS, HEAD_DIM), 1.0)
    cache_v = nrm(ks[3], (DEC_BATCH, PAST_LEN, N_HEADS, HEAD_DIM), DN_BETA)
    state_conv = nrm(ks[4], (N_A, DEC_BATCH, CONV_W - 1, D_MODEL), 1.0)
    ln_g = 1.0 + nrm(ks[5], (DEPTH, 3, D_MODEL), 0.01)
    ln_b = nrm(ks[6], (DEPTH, 3, D_MODEL), 0.01)
    w_ffn_gate = nrm(ks[7], (DEPTH, 2, D_MODEL, D_FF), D_MODEL ** -0.5)
    w_ffn_up = nrm(ks[8], (DEPTH, 2, D_MODEL, D_FF), D_MODEL ** -0.5)
    w_ffn_down = nrm(ks[9], (DEPTH, 2, D_FF, D_MODEL), DN_BETA * D_FF ** -0.5)
    w_conv_in = nrm(ks[10], (N_A, D_MODEL, 3 * D_MODEL), D_MODEL ** -0.5)
    w_conv = nrm(ks[11], (N_A, CONV_W, D_MODEL), CONV_W ** -0.5)
    w_conv_out = nrm(ks[12], (N_A, D_MODEL, D_MODEL), DN_BETA * D_MODEL ** -0.5)
    kk, kv = jax.random.split(ks[13])
    w_kv = jnp.concatenate([nrm(kk, (D_MODEL, D_MODEL), D_MODEL ** -0.5),
                            nrm(kv, (D_MODEL, D_MODEL), DN_BETA * D_MODEL ** -0.5)], axis=-1)
    w_q = nrm(ks[14], (N_B, D_MODEL, D_MODEL), D_MODEL ** -0.5)
    w_o = nrm(ks[15], (N_B, D_MODEL, D_MODEL), DN_BETA * D_MODEL ** -0.5)
    return {"x_prompt": x_prompt, "x_sample": x_sample, "cache_k": cache_k, "cache_v": cache_v,
            "state_conv": state_conv, "ln_g": ln_g, "ln_b": ln_b, "w_ffn_gate": w_ffn_gate,
            "w_ffn_up": w_ffn_up, "w_ffn_down": w_ffn_down, "w_conv_in": w_conv_in, "w_conv": w_conv,
            "w_conv_out": w_conv_out, "w_kv": w_kv, "w_q": w_q, "w_o": w_o}


def reference(x_prompt, x_sample, cache_k, cache_v, state_conv, ln_g, ln_b, w_ffn_gate, w_ffn_up,
              w_ffn_down, w_conv_in, w_conv, w_conv_out, w_kv, w_q, w_o):
    weights = (ln_g, ln_b, w_ffn_gate, w_ffn_up, w_ffn_down, w_conv_in, w_conv, w_conv_out, w_kv, w_q, w_o)
    zero_prev = jnp.zeros((N_A, x_prompt.shape[0], CONV_W - 1, D_MODEL), x_prompt.dtype)
    y_prompt, k_prompt, v_prompt, conv_prompt = trunk(x_prompt, zero_prev, None, None, *weights)
    y_sample, k_sample, v_sample, conv_sample = trunk(x_sample, state_conv, cache_k, cache_v, *weights)
    return (y_prompt, y_sample, k_prompt, v_prompt, conv_prompt, k_sample, v_sample, conv_sample)
```

```python
import functools
import numpy as np
import concourse.bass as bass
import concourse.mybir as mybir

F32 = mybir.dt.float32
BF16 = mybir.dt.bfloat16
I32 = mybir.dt.int32
AF = mybir.ActivationFunctionType
ALU = mybir.AluOpType

ENGS = ("pe", "act", "dve", "pool", "sp")
EPOCH = 30000


class Res:
    __slots__ = ("name", "w", "rd", "rd_dma")

    def __init__(self, name=""):
        self.name = name
        self.w = None
        self.rd = {}
        self.rd_dma = []


class Op:
    __slots__ = ("eng", "fn", "deps", "dma", "sig", "sigidx", "dsem", "dtarget", "dprev", "pos", "inc")

    def __init__(self, eng, fn, dma, inc=16):
        self.inc = inc
        self.eng = eng
        self.fn = fn
        self.dma = dma
        self.deps = []
        self.sig = False
        self.sigidx = None
        self.dsem = None
        self.dtarget = None
        self.dprev = None
        self.pos = None


class Prog:
    def __init__(self, nc, n_dsem=None):
        self.nc = nc
        self.ops = {e: [] for e in ENGS}
        self.n_dsem = n_dsem or {"sp": 24, "pool": 12, "act": 4, "cc": 2}
        self.dma_rr = {e: 0 for e in self.n_dsem}
        self.dma_last = {e: [None] * n for e, n in self.n_dsem.items()}
        self.dma_tgt = {e: [0] * n for e, n in self.n_dsem.items()}
        self.all_dma = []

    def op(self, eng, fn, reads=(), writes=(), dma=False, cc=False):
        o = Op(eng, fn, dma, 1 if cc else 16)

        def _flat(xs):
            out = []
            for x_ in xs:
                if isinstance(x_, (list, tuple)):
                    out.extend(_flat(x_))
                elif x_ is not None:
                    out.append(x_)
            return out
        reads = _flat(reads); writes = _flat(writes)
        deps = []
        for r in reads:
            if r.w is not None:
                deps.append(r.w)
        for w in writes:
            if w.w is not None:
                deps.append(w.w)
            deps.extend(w.rd.values())
            deps.extend(w.rd_dma)
        for r in reads:
            if dma:
                r.rd_dma.append(o)
            else:
                r.rd[eng] = o
        for w in writes:
            w.w = o
            w.rd = {}
            w.rd_dma = []
        seen = set()
        for d in deps:
            if d is o or id(d) in seen:
                continue
            seen.add(id(d))
            if (not d.dma) and (not dma) and d.eng == "pe" and eng == "pe":
                continue
            o.deps.append(d)
            if not d.dma:
                d.sig = True
        if dma:
            pl = "cc" if cc else eng
            k = self.dma_rr[pl]
            self.dma_rr[pl] = (k + 1) % self.n_dsem[pl]
            o.dsem = (pl, k)
            o.dprev = self.dma_tgt[pl][k]
            self.dma_tgt[pl][k] += o.inc
            o.dtarget = self.dma_tgt[pl][k]
            self.all_dma.append(o)
        o.pos = len(self.ops[eng])
        self.ops[eng].append(o)
        return o

    def mm(self, out, lhsT, rhs, start, stop, reads, writes, **kw):
        return self.op("pe", lambda e: e.matmul(out, lhsT, rhs, start=start, stop=stop, **kw), reads, writes)

    def tr(self, out, in_, ident, reads, writes):
        return self.op("pe", lambda e: e.transpose(out, in_, ident), reads, writes)

    def act(self, out, in_, func, reads, writes, eng="act", **kw):
        fname = getattr(func, "name", str(func))
        if fname in ("Copy", "Identity"):
            sc = kw.get("scale"); bi = kw.get("bias")
            if sc is None and bi is None:
                return self.op("dve", lambda e: e.tensor_copy(out, in_), reads, writes)
            if bi is None:
                return self.op("dve", lambda e: e.tensor_scalar(out, in_, sc, None, ALU.mult), reads, writes)
            sc2 = 1.0 if sc is None else sc
            return self.op("dve", lambda e: e.tensor_scalar(out, in_, sc2, bi, ALU.mult, ALU.add), reads, writes)
        if fname == "Square":
            return self.op("dve", lambda e: e.tensor_tensor(out, in_, in_, ALU.mult), reads, writes)
        return self.op(eng, lambda e: e.activation(out, in_, func, **kw), reads, writes)

    def tt(self, eng, out, in0, in1, op, reads, writes):
        return self.op(eng, lambda e: e.tensor_tensor(out, in0, in1, op), reads, writes)

    def ts(self, eng, out, in0, s1, s2, op0, op1, reads, writes):
        if op1 is None:
            return self.op(eng, lambda e: e.tensor_scalar(out, in0, s1, None, op0), reads, writes)
        return self.op(eng, lambda e: e.tensor_scalar(out, in0, s1, s2, op0, op1), reads, writes)

    def stt(self, eng, out, in0, scalar, in1, op0, op1, reads, writes):
        return self.op(eng, lambda e: e.scalar_tensor_tensor(out, in0, scalar, in1, op0, op1), reads, writes)

    def copy(self, eng, out, in_, reads, writes):
        if eng == "act":
            eng = "dve"
        return self.op(eng, lambda e: e.tensor_copy(out, in_), reads, writes)

    def memset(self, eng, ap, val, writes):
        return self.op(eng, lambda e: e.memset(ap, val), (), writes)

    def dma(self, q, out, in_, reads, writes, **kw):
        return self.op(q, lambda e: e.dma_start(out=out, in_=in_, **kw), reads, writes, dma=True)

    def emit(self):
        nc = self.nc
        import contextlib
        with contextlib.ExitStack() as st:
            csems = {}
            for e in ("pe", "act", "dve", "pool"):
                n = 0
                for o in self.ops[e]:
                    if o.sig:
                        o.sigidx = n
                        n += 1
                nep = max(1, (n + EPOCH - 1) // EPOCH)
                csems[e] = [st.enter_context(nc.semaphore(f"c_{e}_{i}")) for i in range(nep)]
            dsems = {}
            for q, n in self.n_dsem.items():
                dsems[q] = [st.enter_context(nc.semaphore(f"d_{q}_{i}")) for i in range(n)]
            block = st.enter_context(nc.Block())

            def run_engine(ename, eng):
                waited_c = {}
                waited_d = {}

                def wait_dep(d):
                    if d.dma:
                        key = d.dsem
                        if waited_d.get(key, 0) >= d.dtarget:
                            return
                        waited_d[key] = d.dtarget
                        eng.wait_ge(dsems[key[0]][key[1]], d.dtarget)
                    else:
                        if waited_c.get(d.eng, -1) >= d.sigidx:
                            return
                        waited_c[d.eng] = d.sigidx
                        ep, v = divmod(d.sigidx, EPOCH)
                        eng.wait_ge(csems[d.eng][ep], v + 1)

                for o in self.ops[ename]:
                    for d in o.deps:
                        wait_dep(d)
                    if o.dma:
                        q, k = o.dsem
                        if o.dprev > 0 and waited_d.get(o.dsem, 0) < o.dprev:
                            waited_d[o.dsem] = o.dprev
                            eng.wait_ge(dsems[q][k], o.dprev)
                        ins = o.fn(eng)
                        ins.then_inc(dsems[q][k], o.inc)
                    else:
                        ins = o.fn(eng)
                        if o.sig:
                            ep, v = divmod(o.sigidx, EPOCH)
                            ins.then_inc(csems[ename][ep], 1)
                if ename == "sp":
                    for q, n in self.n_dsem.items():
                        for k in range(n):
                            t = self.dma_tgt[q][k]
                            if t > 0 and waited_d.get((q, k), 0) < t:
                                eng.wait_ge(dsems[q][k], t)

            @block.tensor
            def _(e):
                run_engine("pe", e)

            @block.scalar
            def _(e):
                run_engine("act", e)

            @block.vector
            def _(e):
                run_engine("dve", e)

            @block.gpsimd
            def _(e):
                run_engine("pool", e)

            @block.sync
            def _(e):
                run_engine("sp", e)


import contextlib
from concourse.bass_utils import run_bass_kernel_spmd

D = 1024
NCH = 8
FF = 2816
NF = 22
MT = 512
SM = 52
NTOK = 2048 + SM
ALPHA = 8.0 ** 0.25
EPS = 1e-5
A_COLS = (18, 36)


def build(n_pass=4, do_phase2=True, stage=99):
    nc = bass.Bass("TRN2", target_bir_lowering=False)

    import os as _os
    TINY = _os.environ.get("KTINY", "") == "1"

    def din(name, shape, dt=F32):
        if TINY and (name.startswith("w_") or name.startswith("cache")):
            shape = [2] * len(shape)
        return nc.dram_tensor(name, shape, dt, kind="ExternalInput").ap()

    def dout(name, shape, dt=F32):
        return nc.dram_tensor(name, shape, dt, kind="ExternalOutput").ap()

    def dint(name, shape, dt):
        return nc.dram_tensor(name, shape, dt).ap()

    xin = din("xin", [NTOK, D])
    prm = din("prm", [38, D])
    masks_d = din("masks", [128, 16 * 512])
    smask_d = din("smask", [SM, 512])
    hmask_d = din("hmask", [128, 1])
    ck_d = din("cache_k", [2, 2048, D])
    cv_d = din("cache_v", [2, 2048, D])
    w_gate = din("w_ffn_gate", [4, 2, D, FF])
    w_up = din("w_ffn_up", [4, 2, D, FF])
    w_down = din("w_ffn_down", [4, 2, FF, D])
    w_cin = din("w_conv_in", [2, D, 3 * D])
    w_cout = din("w_conv_out", [2, D, D])
    w_kv = din("w_kv", [D, 2 * D])
    w_q = din("w_q", [2, D, D])
    w_o = din("w_o", [2, D, D])
    y_out = dout("y_out", [NTOK, D])
    k_out = dout("k_out", [NTOK, D])
    v_out = dout("v_out", [NTOK, D])
    conv_out = dout("conv_out", [2, 6, D])
    kT_in = [dint(f"kT_in{m_}", [D, MT], BF16) for m_ in range(4)]
    v_in = [dint(f"v_in{m_}", [MT, D], BF16) for m_ in range(4)]
    kT_all = [dint(f"kT_all{m_}", [4 * D, MT], BF16) for m_ in range(4)]
    v_all = [dint(f"v_all{m_}", [4 * MT, D], BF16) for m_ in range(4)]
    xa_d = dint("xa_d", [128, NCH * NTOK], F32)
    xb_d = dint("xb_d", [128, NCH * NTOK], BF16)

    with contextlib.ExitStack() as st:
        def sb(name, shape, dt):
            return st.enter_context(nc.sbuf_tensor(name, shape, dt))

        NMAX = 576
        P = Prog(nc)
        ps = [st.enter_context(nc.psum_tensor(f"ps{i}", [128, 512], F32)) for i in range(8)]
        Rps = [Res(f"ps{i}") for i in range(8)]

        def v3(t, n):
            return t[:, 0:NCH * n].rearrange("p (c n) -> p c n", c=NCH)

        xa_t = sb("xa", [128, NCH * NMAX], F32); xa = v3(xa_t, NMAX); Rxa = [Res(f"xa{c_}") for c_ in range(NCH)]
        xb_t = sb("xb", [128, NCH * NMAX], BF16); xb = v3(xb_t, NMAX); Rxb = Res("xb")
        r = xa; Rr = Rxa; rb = xb; Rrb = Rxb
        rq_t = sb("rq", [128, NCH * NMAX], BF16); rq = v3(rq_t, NMAX); Rrq = Res("rq")
        mean_s = sb("mean_s", [128, NMAX], F32); Rmean = Res("mean")
        rstd_s = sb("rstd_s", [128, NMAX], F32); Rrstd = Res("rstd")
        m2_s = sb("m2_s", [128, NMAX], F32); Rm2 = Res("m2")
        ident = sb("ident", [128, 128], F32); Rid = Res("ident")
        uneg = sb("uneg", [128, 128], BF16); ubar = sb("ubar", [128, 128], BF16)
        nones = sb("nones", [128, 128], BF16); meanm = sb("meanm", [128, 128], BF16)
        Rconst = Res("const")
        par_t = sb("par", [128, NCH * 38], F32); par = par_t[:, :].rearrange("p (c n) -> p c n", c=NCH); Rpar = Res("par")
        apar_t = sb("apar", [128, NCH * 24], F32); apar = apar_t[:, :].rearrange("p (c n) -> p c n", c=NCH)
        hmask = sb("hmask_sb", [128, 1], F32)
        masks = sb("masks_sb", [128, 16 * 512], BF16); Rmask = Res("masks")
        smask = sb("smask_sb", [SM, 512], BF16)
        stg = [sb(f"stg{i}", [128, D], F32) for i in range(2)]; Rstg = [Res("stg0"), Res("stg1")]
        ubm2 = [sb(f"ubm{i}", [128, 514], F32) for i in range(2)]; Rubm2 = [Res("ubm0"), Res("ubm1")]
        ulast_t = sb("ulast", [128, NCH * 32], F32); ulast = v3(ulast_t, 32); Rulast = Res("ulast")
        ubs_t = sb("ubs", [128, NCH * 96], F32); ubs = v3(ubs_t, 96); Rubs = Res("ubs")
        carry_t = sb("carry", [128, 2 * NCH * 8], F32)
        carry = carry_t[:, :].rearrange("p (l c m t) -> p l c m t", l=2, c=NCH, m=4); Rcarry = Res("carry")
        tmpf = [sb(f"tmpf{i}", [128, 512], F32) for i in range(2)]; Rtmpf = [Res("tf0"), Res("tf1")]
        ctmp = [sb(f"ctmp{i}", [128, 512], F32) for i in range(2)]; Rctmp = [Res("ct0"), Res("ct1")]
        kTs_t = sb("kTs", [128, NCH * SM], BF16); kTs = v3(kTs_t, SM); RkTs = Res("kTs")
        vs = sb("vs", [SM, D], BF16); Rvs = Res("vs")
        cst = sb("cst", [128, 16 * 128], F32); Rcst2 = [Res("cst0"), Res("cst1")]
        AR = sb("arena", [128, 52224], BF16)

        arena_res = []

        def arena_switch(layout):
            old_ops = []
            for rr in arena_res:
                if rr.w is not None:
                    old_ops.append(rr.w)
                old_ops.extend(rr.rd.values())
                old_ops.extend(rr.rd_dma)
            latest = {}
            dmas = []
            seen = set()
            for o in old_ops:
                if id(o) in seen:
                    continue
                seen.add(id(o))
                if o.dma:
                    dmas.append(o)
                else:
                    if o.eng not in latest or latest[o.eng].pos < o.pos:
                        latest[o.eng] = o
            fence = list(latest.values()) + dmas
            del arena_res[:]
            aps, res = {}, {}
            off = 0
            for name, n in layout:
                aps[name] = AR[:, off:off + n]
                rr = Res(name)
                rr.rd_dma = list(fence)
                res[name] = rr
                arena_res.append(rr)
                off += n
            assert off <= 52224, off
            return aps, res

        psc = {"i": 0}

        P.memset("pool", stg[0][:], 0.0, [Rstg[0]])
        P.memset("pool", stg[1][:], 0.0, [Rstg[1]])
        P.memset("pool", xa_t[:], 0.0, [Rxa])
        P.memset("pool", xb_t[:], 0.0, [Rxb])
        P.memset("pool", ubs_t[:], 0.0, [Rubs])
        P.memset("pool", ulast_t[:], 0.0, [Rulast])
        P.memset("pool", ident[:], 0.0, [Rid])
        P.op("pool", lambda e: e.affine_select(ident[:], ident[:], [[-1, 128]], ALU.not_equal, 1.0, base=0, channel_multiplier=1), [Rid], [Rid])
        import os
        SK = os.environ.get("KSKIP", "").split(",")
        if "consts" not in SK:
            P.memset("pool", uneg[:], -1.0, [Rconst])
            P.op("pool", lambda e: e.affine_select(uneg[:], uneg[:], [[-1, 128]], ALU.is_ge, 0.0, base=0, channel_multiplier=1), [Rconst], [Rconst])
            P.memset("pool", nones[:], -1.0, [Rconst])
            P.tt("pool", ubar[:], nones[:], uneg[:], ALU.subtract, [Rconst], [Rconst])
            P.memset("pool", meanm[:], 1.0 / D, [Rconst])
        if "masks" not in SK:
            for mi_ in range(16):
                P.dma("pool", masks[:, mi_ * 512:(mi_ + 1) * 512], masks_d[:, mi_ * 512:(mi_ + 1) * 512], [], [Rmask])
        if "smask" not in SK:
            P.dma("pool", smask[:], smask_d[:, :], [], [Rmask])
        if "hmask" not in SK:
            P.dma("sp", hmask[:], hmask_d[:, :], [], [Rpar])

        def load_tok(rows_ap, T, dst3, t0, Rdst, scale_dst=None):
            k = psc["i"] % 2; psc["i"] += 1
            Tp = ((T + 31) // 32) * 32
            P.dma("sp", stg[k][0:T, :], rows_ap, [], [Rstg[k]])
            for half in range(2):
                bank = 6 + half
                for cc in range(4):
                    c = half * 4 + cc
                    P.tr(ps[bank][:, cc * 128:cc * 128 + Tp], stg[k][0:Tp, c * 128:(c + 1) * 128], ident[0:Tp, 0:Tp],
                         [Rstg[k], Rid], [Rps[bank]])
                src = ps[bank][:, :].rearrange("p (c n) -> p c n", c=4)[:, :, 0:T]
                for (d3, Rd, eng, sc) in dst3:
                    dd = d3[:, half * 4:half * 4 + 4, t0:t0 + T]
                    if eng == "act":
                        if sc is None:
                            P.act(dd, src, AF.Copy, [Rps[bank]], [Rd])
                        else:
                            P.act(dd, src, AF.Identity, [Rps[bank]], [Rd], scale=sc)
                    else:
                        if sc is None:
                            P.copy(eng, dd, src, [Rps[bank]], [Rd])
                        else:
                            P.ts(eng, dd, src, sc, None, ALU.mult, None, [Rps[bank]], [Rd])

        def store_tok(src3, t0, T, rows_ap, Rsrc):
            k = psc["i"] % 2; psc["i"] += 1
            Tp = ((T + 31) // 32) * 32
            for half in range(2):
                bank = 6 + half
                for cc in range(4):
                    c = half * 4 + cc
                    P.tr(ps[bank][0:Tp, cc * 128:(cc + 1) * 128], src3[:, c, t0:t0 + Tp], ident[:, :],
                         [Rsrc, Rid], [Rps[bank]])
                if half == 0:
                    P.copy("dve", stg[k][0:T, 0:512], ps[bank][0:T, :], [Rps[bank]], [Rstg[k]])
                else:
                    P.act(stg[k][0:T, 512:1024], ps[bank][0:T, :], AF.Copy, [Rps[bank]], [Rstg[k]])
            P.dma("sp", rows_ap, stg[k][0:T, :], [Rstg[k]], [Res()])

        if "params" not in SK:
            load_tok(prm[:, :], 38, [(par, Rpar, "dve", None)], 0, Rpar)
            P.ts("dve", apar[:, :, :], par[:, :, 0:24], ALPHA, None, ALU.mult, None, [Rpar], [Rpar])

        def gcol(c, idx):
            return par[:, c, idx:idx + 1]

        def ln_prep(dm, n0, n):
            P.copy("dve", xb[:, dm, n0:n0 + n], xa[:, dm, n0:n0 + n], [Rxa[dm]], [Rxb])
            P.tt("dve", rq[:, dm, n0:n0 + n], xa[:, dm, n0:n0 + n], xa[:, dm, n0:n0 + n], ALU.mult, [Rxa[dm]], [Rrq])

        def layernorm(lidx, chunks, N, last=False):
            for (n0, n) in chunks:
                for c in range(NCH):
                    P.mm(ps[6][:, 0:n], meanm[:], rb[:, c, n0:n0 + n], c == 0, c == NCH - 1, [Rconst, Rrb], [Rps[6]])
                for c in range(NCH):
                    P.mm(ps[7][:, 0:n], meanm[:], rq[:, c, n0:n0 + n], c == 0, c == NCH - 1, [Rconst, Rrq], [Rps[7]])
                P.act(mean_s[:, n0:n0 + n], ps[6][:, 0:n], AF.Copy, [Rps[6]], [Rmean])
                P.tt("dve", m2_s[:, n0:n0 + n], mean_s[:, n0:n0 + n], mean_s[:, n0:n0 + n], ALU.mult, [Rmean], [Rm2])
                P.tt("dve", m2_s[:, n0:n0 + n], ps[7][:, 0:n], m2_s[:, n0:n0 + n], ALU.subtract, [Rps[7], Rm2], [Rm2])
                P.ts("dve", m2_s[:, n0:n0 + n], m2_s[:, n0:n0 + n], 0.0, EPS, ALU.max, ALU.add, [Rm2], [Rm2])
                P.act(rstd_s[:, n0:n0 + n], m2_s[:, n0:n0 + n], AF.Ln, [Rm2], [Rrstd])
                P.act(rstd_s[:, n0:n0 + n], rstd_s[:, n0:n0 + n], AF.Exp, [Rrstd], [Rrstd], scale=-0.5)
            for c in range(NCH):
                P.tt("dve", r[:, c, 0:N], r[:, c, 0:N], mean_s[:, 0:N], ALU.subtract, [Rr[c], Rmean], [Rr[c]])
                P.tt("dve", r[:, c, 0:N], r[:, c, 0:N], rstd_s[:, 0:N], ALU.mult, [Rr[c], Rrstd], [Rr[c]])
                P.act(xb[:, c, 0:N], r[:, c, 0:N], AF.Identity, [Rr[c], Rpar], [Rxb],
                      scale=par[:, c, lidx:lidx + 1], bias=par[:, c, 12 + lidx:13 + lidx])
                if last:
                    P.ts("pool", xa[:, c, 0:N], r[:, c, 0:N], par[:, c, lidx:lidx + 1], par[:, c, 12 + lidx:13 + lidx],
                         ALU.mult, ALU.add, [Rr[c], Rpar], [Rxa[c]])
                else:
                    P.ts("pool", xa[:, c, 0:N], r[:, c, 0:N], apar[:, c, lidx:lidx + 1], apar[:, c, 12 + lidx:13 + lidx],
                         ALU.mult, ALU.add, [Rr[c], Rpar], [Rxa[c]])

        def ffn(l, i, chunks, N):
            NBG, NBD = 4, 3
            aps, res = arena_switch([(f"wg{i_}", 2048) for i_ in range(NBG)] + [(f"wu{i_}", 2048) for i_ in range(NBG)] +
                                    [(f"wd{i_}", 5632) for i_ in range(NBD)] + [("hT", NF * NMAX)])
            wg = [aps[f"wg{i_}"].rearrange("p (c f) -> p c f", c=NCH) for i_ in range(NBG)]
            wu = [aps[f"wu{i_}"].rearrange("p (c f) -> p c f", c=NCH) for i_ in range(NBG)]
            wd = [aps[f"wd{i_}"].rearrange("p (f d) -> p f d", f=NF) for i_ in range(NBD)]
            hT = aps["hT"].rearrange("p (f n) -> p f n", f=NF)
            Rwg = [res[f"wg{i_}"] for i_ in range(NBG)]; Rwu = [res[f"wu{i_}"] for i_ in range(NBG)]
            Rwd = [res[f"wd{i_}"] for i_ in range(NBD)]; RhT = res["hT"]
            WG = w_gate[l, i].rearrange("(c p) f -> p c f", p=128)
            WU = w_up[l, i].rearrange("(c p) f -> p c f", p=128)
            WD = w_down[l, i].rearrange("(f p) d -> p f d", p=128)

            def load_gu(s):
                b = s % NBG
                P.dma("pool", wg[b], WG[:, :, s * 256:(s + 1) * 256], [], [Rwg[b]])
                P.dma("pool", wu[b], WU[:, :, s * 256:(s + 1) * 256], [], [Rwu[b]])

            def load_d(s):
                b = s % NBD
                P.dma("pool", wd[b], WD[:, :, s * 256:(s + 1) * 256], [], [Rwd[b]])

            for s_ in range(NBG):
                load_gu(s_)
            yield
            k = 0
            for s in range(11):
                b = s % NBG
                for fc in range(2):
                    f = 2 * s + fc
                    for (n0, n) in chunks:
                        gbk = k % 2; ubk = 2 + k % 2; k += 1
                        for c in range(NCH):
                            P.mm(ps[gbk][:, 0:n], wg[b][:, c, fc * 128:(fc + 1) * 128], xb[:, c, n0:n0 + n],
                                 c == 0, c == NCH - 1, [Rwg[b], Rxb], [Rps[gbk]])
                        for c in range(NCH):
                            P.mm(ps[ubk][:, 0:n], wu[b][:, c, fc * 128:(fc + 1) * 128], xb[:, c, n0:n0 + n],
                                 c == 0, c == NCH - 1, [Rwu[b], Rxb], [Rps[ubk]])
                        tf = k % 2
                        P.act(tmpf[tf][:, 0:n], ps[gbk][:, 0:n], AF.Silu, [Rps[gbk]], [Rtmpf[tf]])
                        P.tt("dve", hT[:, f, n0:n0 + n], tmpf[tf][:, 0:n], ps[ubk][:, 0:n], ALU.mult,
                             [Rtmpf[tf], Rps[ubk]], [RhT])
                if s + NBG < 11:
                    load_gu(s + NBG)
                if s < NBD:
                    load_d(s)
            for s in range(4):
                b = s % NBD
                for dc in range(2):
                    dm = 2 * s + dc
                    for (n0, n) in chunks:
                        yb = 4 + k % 2; k += 1
                        for f in range(NF):
                            P.mm(ps[yb][:, 0:n], wd[b][:, f, dc * 128:(dc + 1) * 128], hT[:, f, n0:n0 + n],
                                 f == 0, f == NF - 1, [Rwd[b], RhT], [Rps[yb]])
                        P.stt("dve", r[:, dm, n0:n0 + n], ps[yb][:, 0:n], 0.5, xa[:, dm, n0:n0 + n], ALU.mult, ALU.add,
                              [Rps[yb], Rxa[dm]], [Rr[dm]])
                        ln_prep(dm, n0, n)
                if s + NBD < 4:
                    load_d(s + NBD)

        def conv(l, m, chunks, N):
            aps, res = arena_switch([("win0", 3072), ("win1", 3072), ("gT", NCH * NMAX), ("wo0", 2048), ("wo1", 2048)])
            win = [aps["win0"].rearrange("p (c g f) -> p c g f", c=NCH, g=3), aps["win1"].rearrange("p (c g f) -> p c g f", c=NCH, g=3)]
            gT = aps["gT"].rearrange("p (c n) -> p c n", c=NCH)
            wo = [aps["wo0"].rearrange("p (c f) -> p c f", c=NCH), aps["wo1"].rearrange("p (c f) -> p c f", c=NCH)]
            Rwin = [res["win0"], res["win1"]]; RgT = res["gT"]; Rwo = [res["wo0"], res["wo1"]]
            WI = w_cin[l].rearrange("(c p) (g i f) -> p c g i f", p=128, g=3, i=NCH)
            WO = w_cout[l].rearrange("(c p) d -> p c d", p=128)

            def load_in(i):
                b = i % 2
                for g in range(3):
                    P.dma("pool", win[b][:, :, g, :], WI[:, :, g, i, :], [], [Rwin[b]])

            def load_o(s):
                P.dma("pool", wo[s % 2], WO[:, :, s * 256:(s + 1) * 256], [], [Rwo[s % 2]])

            load_in(0)
            load_in(1)
            load_o(0)
            load_o(1)
            yield
            order = list(reversed(chunks))
            if m == 0:
                P.memset("pool", gT[:, :, MT:MT + 2], 0.0, [RgT])
            k = 0
            for i in range(NCH):
                b = i % 2
                w0 = par[:, i, 24 + 3 * l:25 + 3 * l]; w1 = par[:, i, 25 + 3 * l:26 + 3 * l]; w2 = par[:, i, 26 + 3 * l:27 + 3 * l]
                for (n0, n) in order:
                    small = n0 >= MT
                    base = 3 * (k % 2); k += 1
                    for g in range(3):
                        bank = base + g
                        for c in range(NCH):
                            P.mm(ps[bank][:, 0:n], win[b][:, c, g, :], xb[:, c, n0:n0 + n], c == 0, c == NCH - 1,
                                 [Rwin[b], Rxb], [Rps[bank]])
                    tf = k % 2
                    P.act(tmpf[tf][:, 0:n], ps[base + 1][:, 0:n], AF.Copy, [Rps[base + 1]], [Rtmpf[tf]])
                    ct = ctmp[tf]; Rct = Rctmp[tf]
                    if small:
                        P.tt("dve", ubs[:, i, 0:SM], tmpf[tf][:, 0:n], ps[base + 2][:, 0:n], ALU.mult,
                             [Rtmpf[tf], Rps[base + 2]], [Rubs])
                        for s_ in range(2):
                            pc = A_COLS[s_] - 2
                            P.copy("pool", ubs[:, i, pc:pc + 2], par[:, i, 30 + 4 * l + 2 * s_:32 + 4 * l + 2 * s_], [Rpar], [Rubs])
                        nn = SM - 2
                        P.ts("pool", ct[:, 0:nn], ubs[:, i, 0:nn], w0, None, ALU.mult, None, [Rubs, Rpar], [Rct])
                        P.stt("dve", ct[:, 0:nn], ubs[:, i, 1:nn + 1], w1, ct[:, 0:nn], ALU.mult, ALU.add, [Rubs, Rpar, Rct], [Rct])
                        P.stt("dve", ct[:, 0:nn], ubs[:, i, 2:nn + 2], w2, ct[:, 0:nn], ALU.mult, ALU.add, [Rubs, Rpar, Rct], [Rct])
                        P.tt("dve", gT[:, i, MT + 2:MT + SM], ct[:, 0:nn], ps[base][:, 2:SM], ALU.mult, [Rct, Rps[base]], [RgT])
                        for mm_ in range(4):
                            if mm_ == 0:
                                P.ts("pool", carry[:, l, i, 0, :], ubs[:, i, 2:4], hmask[:, 0:1], None, ALU.mult, None, [Rubs, Rpar], [Rcarry])
                            else:
                                P.copy("pool", carry[:, l, i, mm_, :], ubs[:, i, 4 * mm_ + 2:4 * mm_ + 4], [Rubs], [Rcarry])
                    else:
                        ubm = ubm2[i % 2]; Rubm = Rubm2[i % 2]
                        P.copy("pool", ubm[:, 0:2], carry[:, l, i, m, :], [Rcarry], [Rubm])
                        P.tt("dve", ubm[:, 2:514], tmpf[tf][:, 0:n], ps[base + 2][:, 0:n], ALU.mult,
                             [Rtmpf[tf], Rps[base + 2]], [Rubm])
                        P.copy("pool", ulast[:, i, 0:2], ubm[:, 512:514], [Rubm], [Rulast])
                        P.ts("pool", ct[:, 0:n], ubm[:, 0:n], w0, None, ALU.mult, None, [Rubm, Rpar], [Rct])
                        P.stt("dve", ct[:, 0:n], ubm[:, 1:n + 1], w1, ct[:, 0:n], ALU.mult, ALU.add, [Rubm, Rpar, Rct], [Rct])
                        P.stt("dve", ct[:, 0:n], ubm[:, 2:n + 2], w2, ct[:, 0:n], ALU.mult, ALU.add, [Rubm, Rpar, Rct], [Rct])
                        P.tt("dve", gT[:, i, 0:n], ct[:, 0:n], ps[base][:, 0:n], ALU.mult, [Rct, Rps[base]], [RgT])
                if i + 2 < NCH:
                    load_in(i + 2)
            for s in range(4):
                b = s % 2
                for dc in range(2):
                    dm = 2 * s + dc
                    for (n0, n) in chunks:
                        yb = 6 + k % 2; k += 1
                        for c in range(NCH):
                            P.mm(ps[yb][:, 0:n], wo[b][:, c, dc * 128:(dc + 1) * 128], gT[:, c, n0:n0 + n], c == 0, c == NCH - 1,
                                 [Rwo[b], RgT], [Rps[yb]])
                        P.tt("dve", r[:, dm, n0:n0 + n], ps[yb][:, 0:n], xa[:, dm, n0:n0 + n], ALU.add, [Rps[yb], Rxa[dm]], [Rr[dm]])
                        ln_prep(dm, n0, n)
                if s + 2 < 4:
                    load_o(s + 2)
            if m == 3:
                store_tok(ulast, 0, 2, conv_out[l, 0:2, :], Rulast)
            if m == 0:
                store_tok(ubs, A_COLS[0] + 14, 2, conv_out[l, 2:4, :], Rubs)
                store_tok(ubs, A_COLS[1] + 14, 2, conv_out[l, 4:6, :], Rubs)

        RkTin = [Res(f"kTin{m}") for m in range(4)]
        Rvin = [Res(f"vin{m}") for m in range(4)]

        def kvproj(m, chunks, N):
            aps, res = arena_switch([("wk0", 2048), ("wk1", 2048), ("wt0", 4096), ("wt1", 4096), ("ktl", NCH * NMAX),
                                     ("vb0", 512), ("vb1", 512)])
            wk = [aps["wk0"].rearrange("p (c f) -> p c f", c=NCH), aps["wk1"].rearrange("p (c f) -> p c f", c=NCH)]
            wt = [aps["wt0"].rearrange("p (c f) -> p c f", c=NCH), aps["wt1"].rearrange("p (c f) -> p c f", c=NCH)]
            ktl = aps["ktl"].rearrange("p (c n) -> p c n", c=NCH)
            vb = [aps["vb0"], aps["vb1"]]
            Rwk = [res["wk0"], res["wk1"]]; Rwt = [res["wt0"], res["wt1"]]; Rktl = res["ktl"]; Rvb = [res["vb0"], res["vb1"]]
            WKV = w_kv.rearrange("(c p) d -> p c d", p=128)
            P.dma("pool", wk[0], WKV[:, :, 0:256], [], [Rwk[0]])
            P.dma("pool", wk[1], WKV[:, :, 256:512], [], [Rwk[1]])
            P.dma("pool", wt[0], WKV[:, :, 0:512], [], [Rwt[0]])
            P.dma("pool", wt[1], WKV[:, :, 512:1024], [], [Rwt[1]])
            yield
            k = 0
            for s in range(4):
                b = s % 2
                for dc in range(2):
                    dm = 2 * s + dc
                    for (n0, n) in chunks:
                        yb = 4 + k % 2; k += 1
                        for c in range(NCH):
                            P.mm(ps[yb][:, 0:n], wk[b][:, c, dc * 128:(dc + 1) * 128], xb[:, c, n0:n0 + n], c == 0, c == NCH - 1,
                                 [Rwk[b], Rxb], [Rps[yb]])
                        P.act(ktl[:, dm, n0:n0 + n], ps[yb][:, 0:n], AF.Copy, [Rps[yb]], [Rktl])
                if s + 2 < 4:
                    P.dma("pool", wk[b], WKV[:, :, (s + 2) * 256:(s + 3) * 256], [], [Rwk[b]])
            P.dma("sp", kT_in[m].rearrange("(c p) t -> p c t", p=128), ktl[:, :, 0:MT], [Rktl], [RkTin[m]])
            if m == 0:
                P.copy("pool", kTs[:, :, :], ktl[:, :, MT:MT + SM], [Rktl], [RkTs])
            blocks = [(tb * 128, 128) for tb in range(4)] + ([(MT, SM)] if m == 0 else [])
            for g in range(4):
                b = g % 2
                outd = k_out if g < 2 else v_out
                cs = (g % 2) * 512
                for (t0, T) in blocks:
                    bank = k % 2; k += 1
                    for c in range(NCH):
                        P.mm(ps[bank][0:T, :], xb[:, c, t0:t0 + T], wt[b][:, c, :], c == 0, c == NCH - 1, [Rxb, Rwt[b]], [Rps[bank]])
                    tf = k % 2
                    P.act(tmpf[tf][0:T, :], ps[bank][0:T, :], AF.Copy, [Rps[bank]], [Rtmpf[tf]])
                    row0 = (MT * m + t0) if t0 < MT else (2048 + t0 - MT)
                    P.dma("sp", outd[row0:row0 + T, cs:cs + 512], tmpf[tf][0:T, :], [Rtmpf[tf]], [Res()])
                    if g >= 2:
                        if t0 < MT:
                            P.copy("dve", vb[tf][0:T, :], ps[bank][0:T, :], [Rps[bank]], [Rvb[tf]])
                            P.dma("sp", v_in[m][t0:t0 + T, cs:cs + 512], vb[tf][0:T, :], [Rvb[tf]], [Rvin[m]])
                        else:
                            P.copy("dve", vs[0:T, cs:cs + 512], ps[bank][0:T, :], [Rps[bank]], [Rvs])
                if g + 2 < 4:
                    P.dma("pool", wt[b], WKV[:, :, (g + 2) * 512:(g + 3) * 512], [], [Rwt[b]])

        RkTall = [Res(f"kTall{m_}") for m_ in range(4)]; Rvall = [Res(f"vall{m_}") for m_ in range(4)]

        def attn(l, m, chunks, N):
            li = l - 2
            aps, res = arena_switch([("qT", NCH * NMAX), ("oT", NCH * NMAX), ("kT0", 8192), ("kT1", 8192), ("vv0", 8192), ("vv1", 8192),
                                     ("e00", 512), ("e01", 512), ("e10", 512), ("e11", 512),
                                     ("sp00", 512), ("sp01", 512), ("sp10", 512), ("sp11", 512),
                                     ("ex0", 512), ("ex1", 512), ("ww0", 512), ("ww1", 512), ("wq0", 2048), ("wq1", 2048)])
            qT = aps["qT"].rearrange("p (c n) -> p c n", c=NCH); RqT = res["qT"]
            oT = aps["oT"].rearrange("p (c n) -> p c n", c=NCH); RoT = res["oT"]
            kTb = [aps["kT0"], aps["kT1"]]; RkTb = [res["kT0"], res["kT1"]]
            vvb = [aps["vv0"].rearrange("p (b d) -> p b d", d=128), aps["vv1"].rearrange("p (b d) -> p b d", d=128)]
            Rvvb = [res["vv0"], res["vv1"]]
            eb = [[aps["e00"], aps["e01"]], [aps["e10"], aps["e11"]]]
            Reb = [[res["e00"], res["e01"]], [res["e10"], res["e11"]]]
            spb = [[aps["sp00"], aps["sp01"]], [aps["sp10"], aps["sp11"]]]
            Rspb = [[res["sp00"], res["sp01"]], [res["sp10"], res["sp11"]]]
            exb = [aps["ex0"], aps["ex1"]]; Rexb = [res["ex0"], res["ex1"]]
            wwb = [aps["ww0"], aps["ww1"]]; Rwwb = [res["ww0"], res["ww1"]]
            wq = [aps["wq0"].rearrange("p (c f) -> p c f", c=NCH), aps["wq1"].rearrange("p (c f) -> p c f", c=NCH)]
            Rwq = [res["wq0"], res["wq1"]]
            WQ = w_q[li].rearrange("(c p) d -> p c d", p=128)
            WO = w_o[li].rearrange("(c p) d -> p c d", p=128)
            P.dma("pool", wq[0], WQ[:, :, 0:256], [], [Rwq[0]])
            P.dma("pool", wq[1], WQ[:, :, 256:512], [], [Rwq[1]])
            yield
            k = 0
            for s in range(4):
                b = s % 2
                for dc in range(2):
                    dm = 2 * s + dc
                    for (n0, n) in chunks:
                        yb = 6 + k % 2; k += 1
                        for c in range(NCH):
                            P.mm(ps[yb][:, 0:n], wq[b][:, c, dc * 128:(dc + 1) * 128], xb[:, c, n0:n0 + n], c == 0, c == NCH - 1,
                                 [Rwq[b], Rxb], [Rps[yb]])
                        P.act(qT[:, dm, n0:n0 + n], ps[yb][:, 0:n], AF.Identity, [Rps[yb]], [RqT], scale=0.125)
                if s + 2 < 4:
                    P.dma("pool", wq[b], WQ[:, :, (s + 2) * 256:(s + 3) * 256], [], [Rwq[b]])
                else:
                    P.dma("pool", wq[b], WO[:, :, (s - 2) * 256:(s - 1) * 256], [], [Rwq[b]])
            if m == 0:
                P.memset("pool", oT[:, :, MT:MT + SM], 0.0, [RoT])

            def emit_chains(blocks, q_aps, nq):
                nb = len(blocks)

                def zmm(a):
                    blk = blocks[a]; K = blk["K"]; pr = a % 2
                    for hh in range(2):
                        Z = 2 * hh + pr
                        P.mm(ps[Z][0:K, 0:nq], blk["kT"][hh], q_aps[hh], True, True, [blk["Rk"], RqT], [Rps[Z]])

                zmm(0)
                for t in range(nb + 1):
                    b = t - 1
                    if b >= 0:
                        bb = blocks[b]; Kb = bb["K"]; pb = b % 2
                        for hh in range(2):
                            X = 4 + hh
                            P.mm(ps[X][0:Kb, 0:nq], uneg[0:Kb, 0:Kb], spb[hh][pb][0:Kb, 0:nq], b == 0, True,
                                 [Rconst, Rspb[hh][pb]], [Rps[X]], skip_group_check=True)
                    if t + 1 < nb:
                        zmm(t + 1)
                    if t < nb:
                        blk = blocks[t]; K = blk["K"]; pr = t % 2
                        for hh in range(2):
                            Z = 2 * hh + pr
                            P.act(eb[hh][pr][0:K, 0:nq], ps[Z][0:K, 0:nq], AF.Exp, [Rps[Z]], [Reb[hh][pr]])
                        if blk["mask"] is not None:
                            for hh in range(2):
                                P.tt("pool", eb[hh][pr][0:K, 0:nq], eb[hh][pr][0:K, 0:nq], blk["mask"], ALU.mult,
                                     [Reb[hh][pr], Rmask], [Reb[hh][pr]])
                    if b >= 0:
                        for hh in range(2):
                            X = 4 + hh
                            P.act(exb[hh][0:Kb, 0:nq], ps[X][0:Kb, 0:nq], AF.Exp, [Rps[X]], [Rexb[hh]])
                    if t < nb:
                        for hh in range(2):
                            P.act(spb[hh][pr][0:K, 0:nq], eb[hh][pr][0:K, 0:nq], AF.Ln, [Reb[hh][pr]], [Rspb[hh][pr]], bias=1.0)
                    if b >= 0:
                        if b < nb - 1:
                            for hh in range(2):
                                X = 4 + hh
                                if bb["restart52"]:
                                    P.mm(ps[X][:, 0:nq], nones[0:Kb, :], spb[hh][pb][0:Kb, 0:nq], True, True,
                                         [Rconst, Rspb[hh][pb]], [Rps[X]], skip_group_check=True)
                                else:
                                    P.mm(ps[X][:, 0:nq], ubar[:, :], spb[hh][pb][:, 0:nq], False, True,
                                         [Rconst, Rspb[hh][pb]], [Rps[X]], skip_group_check=True)
                        for hh in range(2):
                            P.tt("dve", wwb[hh][0:Kb, 0:nq], eb[hh][pb][0:Kb, 0:nq], exb[hh][0:Kb, 0:nq], ALU.mult,
                                 [Reb[hh][pb], Rexb[hh]], [Rwwb[hh]])
                        for hh in range(2):
                            O = 6 + hh
                            P.mm(ps[O][:, 0:nq], bb["v"], wwb[hh][0:Kb, 0:nq], b == 0, b == nb - 1,
                                 [bb["Rv"], Rwwb[hh]], [Rps[O]], skip_group_check=True)

            nkt = 4 * m + 4

            def load_kv(c):
                bf = c % 2
                for lt in range(m + 1):
                    for rk in range(4):
                        kt = 4 * lt + rk
                        P.dma("sp", kTb[bf][:, kt * 512:(kt + 1) * 512], kT_all[lt][rk * D + c * 128:rk * D + (c + 1) * 128, :],
                              [RkTall[lt]], [RkTb[bf]])
                        P.dma("sp", vvb[bf][:, 4 * kt:4 * kt + 4, :],
                              v_all[lt][rk * MT:(rk + 1) * MT, c * 128:(c + 1) * 128].rearrange("(kb p) d -> p kb d", p=128),
                              [Rvall[lt]], [Rvvb[bf]])

            load_kv(0)
            for c in range(NCH):
                if c + 1 < NCH:
                    load_kv(c + 1)
                bf = c % 2; kT = kTb[bf]; vv = vvb[bf]
                nblk = 4 * nkt
                blocks = []
                for bi in range(nblk):
                    B = nblk - 1 - bi
                    kt, kb = divmod(B, 4)
                    mask_ap = None
                    if kt >= 4 * m:
                        mi = (kt - 4 * m) * 4 + kb
                        mask_ap = masks[:, mi * 512:(mi + 1) * 512]
                    blocks.append({"K": 128, "kT": [kT[0:64, B * 128:(B + 1) * 128], kT[64:128, B * 128:(B + 1) * 128]],
                                   "v": vv[:, B, :], "mask": mask_ap, "restart52": False, "Rk": RkTb[bf], "Rv": Rvvb[bf]})
                emit_chains(blocks, [qT[0:64, c, 0:MT], qT[64:128, c, 0:MT]], MT)
                for hh in range(2):
                    ph = slice(64 * hh, 64 * hh + 64)
                    P.copy("dve", oT[ph, c, 0:MT], ps[6 + hh][ph, 0:MT], [Rps[6 + hh]], [RoT])

            if m == 0:
                off_k = NCH * NMAX * 2
                kTbig = AR[:, off_k:off_k + 16384].rearrange("p (c k) -> p c k", c=NCH)
                vbig = AR[:, off_k + 16384:off_k + 32768].rearrange("p (b d) -> p b d", d=D)
                Rkbig = [RkTb[0], RkTb[1]]; Rvbig = [Rvvb[0], Rvvb[1]]
                NQ = 256
                for s_ in range(2):
                    q0 = MT + A_COLS[s_]
                    for kb_ in range(16):
                        for hf_ in range(2):
                            P.dma("pool", vbig[:, kb_, hf_ * 512:(hf_ + 1) * 512],
                                  cv_d[s_, kb_ * 128:(kb_ + 1) * 128, hf_ * 512:(hf_ + 1) * 512], [], [Rvbig])
                    for kb in range(16):
                        h2 = kb % 2
                        P.dma("sp", cst[:, h2 * 1024:(h2 + 1) * 1024], ck_d[s_, kb * 128:(kb + 1) * 128, :], [], [Rcst2[h2]])
                        banks = (5, 7)
                        for half in range(2):
                            bank = banks[half]
                            for cc in range(4):
                                c = half * 4 + cc
                                P.tr(ps[bank][:, cc * 128:(cc + 1) * 128], cst[:, h2 * 1024 + c * 128:h2 * 1024 + (c + 1) * 128],
                                     ident[:, :], [Rcst2[h2], Rid], [Rps[bank]])
                            P.copy("dve", kTbig[:, half * 4:half * 4 + 4, kb * 128:(kb + 1) * 128],
                                   ps[bank][:, :].rearrange("p (c k) -> p c k", c=4), [Rps[bank]], [Rkbig])
                    nb = 17
                    e_ = eb[0]; Re_ = Reb[0]; sp_ = spb[0]; Rsp_ = Rspb[0]; ex_ = exb[0]; Rex_ = Rexb[0]; ww_ = wwb[0]; Rww_ = Rwwb[0]

                    def blkK(a):
                        return SM if a == 0 else 128

                    def zmm_s(a):
                        K = blkK(a)
                        for hh in range(2):
                            Z = 2 * hh + a % 2
                            ph = slice(64 * hh, 64 * hh + 64)
                            for c in range(NCH):
                                if a == 0:
                                    kap = kTs[ph, c, 0:SM]; Rk = RkTs
                                else:
                                    B = 16 - a
                                    kap = kTbig[ph, c, B * 128:(B + 1) * 128]; Rk = Rkbig
                                P.mm(ps[Z][0:K, 16 * c:16 * c + 16], kap, qT[ph, c, q0:q0 + 16], True, True, [Rk, RqT], [Rps[Z]])

                    zmm_s(0)
                    for t in range(nb + 1):
                        b = t - 1
                        if b >= 0:
                            Kb = blkK(b); pb = b % 2
                            P.mm(ps[4][0:Kb, 0:NQ], uneg[0:Kb, 0:Kb], sp_[pb][0:Kb, 0:NQ], b == 0, True,
                                 [Rconst, Rsp_[pb]], [Rps[4]], skip_group_check=True)
                        if t + 1 < nb:
                            zmm_s(t + 1)
                        if t < nb:
                            K = blkK(t); pr = t % 2
                            for hh in range(2):
                                P.act(e_[pr][0:K, 128 * hh:128 * hh + 128], ps[2 * hh + pr][0:K, 0:128], AF.Exp,
                                      [Rps[2 * hh + pr]], [Re_[pr]])
                            if t == 0:
                                P.tt("pool", e_[pr][0:K, 0:NQ], e_[pr][0:K, 0:NQ], smask[0:SM, NQ * s_:NQ * (s_ + 1)], ALU.mult,
                                     [Re_[pr], Rmask], [Re_[pr]])
                        if b >= 0:
                            P.act(ex_[0:Kb, 0:NQ], ps[4][0:Kb, 0:NQ], AF.Exp, [Rps[4]], [Rex_])
                        if t < nb:
                            P.act(sp_[pr][0:K, 0:NQ], e_[pr][0:K, 0:NQ], AF.Ln, [Re_[pr]], [Rsp_[pr]], bias=1.0)
                        if b >= 0:
                            if b < nb - 1:
                                if b == 0:
                                    P.mm(ps[4][:, 0:NQ], nones[0:Kb, :], sp_[pb][0:Kb, 0:NQ], True, True,
                                         [Rconst, Rsp_[pb]], [Rps[4]], skip_group_check=True)
                                else:
                                    P.mm(ps[4][:, 0:NQ], ubar[:, :], sp_[pb][:, 0:NQ], False, True,
                                         [Rconst, Rsp_[pb]], [Rps[4]], skip_group_check=True)
                            P.tt("dve", ww_[0:Kb, 0:NQ].rearrange("p (c h q) -> p c h q", c=NCH, h=2),
                                 e_[pb][0:Kb, 0:NQ].rearrange("p (h c q) -> p c h q", h=2, c=NCH),
                                 ex_[0:Kb, 0:NQ].rearrange("p (h c q) -> p c h q", h=2, c=NCH), ALU.mult, [Re_[pb], Rex_], [Rww_])
                            for c in range(NCH):
                                if b == 0:
                                    vap = vs[0:SM, c * 128:(c + 1) * 128]; Rv = Rvs
                                else:
                                    B = 16 - b
                                    vap = vbig[:, B, c * 128:(c + 1) * 128]; Rv = Rvbig
                                P.mm(ps[6][:, 32 * c:32 * c + 32], vap, ww_[0:Kb, 32 * c:32 * c + 32], b == 0 and c == 0, b == nb - 1,
                                     [Rv, Rww_], [Rps[6]], skip_group_check=True)
                    Oview = ps[6][:, 0:NQ].rearrange("p (c h q) -> p c h q", c=NCH, h=2)
                    P.copy("dve", oT[0:64, :, q0:q0 + 16], Oview[0:64, :, 0, :], [Rps[6]], [RoT])
                    P.copy("dve", oT[64:128, :, q0:q0 + 16], Oview[64:128, :, 1, :], [Rps[6]], [RoT])

            for s in range(4):
                b = s % 2
                for dc in range(2):
                    dm = 2 * s + dc
                    for (n0, n) in chunks:
                        yb = 6 + k % 2; k += 1
                        for c in range(NCH):
                            P.mm(ps[yb][:, 0:n], wq[b][:, c, dc * 128:(dc + 1) * 128], oT[:, c, n0:n0 + n], c == 0, c == NCH - 1,
                                 [Rwq[b], RoT], [Rps[yb]])
                        P.tt("dve", r[:, dm, n0:n0 + n], ps[yb][:, 0:n], xa[:, dm, n0:n0 + n], ALU.add, [Rps[yb], Rxa[dm]], [Rr[dm]])
                        ln_prep(dm, n0, n)
                if s + 2 < 4:
                    P.dma("pool", wq[b], WO[:, :, (s + 2) * 256:(s + 3) * 256], [], [Rwq[b]])

        xa_d3 = xa_d.rearrange("p (c n) -> p c n", c=NCH)
        xb_d3 = xb_d.rearrange("p (c n) -> p c n", c=NCH)
        Rxad = [Res(f"xad{m}") for m in range(4)]
        Rxbd = [Res(f"xbd{m}") for m in range(4)]

        def pass_chunks(m):
            if m == 0:
                return [(0, MT), (MT, SM)], MT + SM
            return [(0, MT)], MT

        steps = []

        def PH(f):
            steps.append(("ph", f))

        def OT(f):
            steps.append(("ot", f))

        def load_x(m):
            for tb in range(4):
                load_tok(xin[MT * m + tb * 128:MT * m + (tb + 1) * 128, :], 128,
                         [(xa, Rxa, "act", ALPHA), (xb, Rxb, "dve", None)], tb * 128, None)
            if m == 0:
                load_tok(xin[2048:2048 + SM, :], SM, [(xa, Rxa, "act", ALPHA), (xb, Rxb, "dve", None)], MT, None)

        def bounce_out(m):
            P.dma("sp", xa_d3[:, :, MT * m:MT * (m + 1)], xa[:, :, 0:MT], [Rxa], [Rxad[m]])
            P.dma("sp", xb_d3[:, :, MT * m:MT * (m + 1)], xb[:, :, 0:MT], [Rxb], [Rxbd[m]])
            if m == 0:
                P.dma("sp", xa_d3[:, :, 2048:2048 + SM], xa[:, :, MT:MT + SM], [Rxa], [Rxad[m]])
                P.dma("sp", xb_d3[:, :, 2048:2048 + SM], xb[:, :, MT:MT + SM], [Rxb], [Rxbd[m]])

        def bounce_in(m):
            P.dma("sp", xa[:, :, 0:MT], xa_d3[:, :, MT * m:MT * (m + 1)], [Rxad[m]], [Rxa])
            P.dma("sp", xb[:, :, 0:MT], xb_d3[:, :, MT * m:MT * (m + 1)], [Rxbd[m]], [Rxb])
            if m == 0:
                P.dma("sp", xa[:, :, MT:MT + SM], xa_d3[:, :, 2048:2048 + SM], [Rxad[m]], [Rxa])
                P.dma("sp", xb[:, :, MT:MT + SM], xb_d3[:, :, 2048:2048 + SM], [Rxbd[m]], [Rxb])

        def store_y(m):
            for tb in range(4):
                store_tok(xa, tb * 128, 128, y_out[MT * m + tb * 128:MT * m + (tb + 1) * 128, :], Rxa)
            if m == 0:
                store_tok(xa, MT, SM, y_out[2048:2048 + SM, :], Rxa)

        def collectives(ms):
            groups = [[0, 1, 2, 3], [4, 5, 6, 7]]
            for m_ in ms:
                P.op("pool", lambda e, m_=m_: e.collective_compute("AllGather", ALU.bypass, replica_groups=groups,
                                                                   ins=[kT_in[m_][:, :]], outs=[kT_all[m_][:, :]]),
                     [RkTin[m_]], [RkTall[m_]], dma=True, cc=True)
                P.op("pool", lambda e, m_=m_: e.collective_compute("AllGather", ALU.bypass, replica_groups=groups,
                                                                   ins=[v_in[m_][:, :]], outs=[v_all[m_][:, :]]),
                     [Rvin[m_]], [Rvall[m_]], dma=True, cc=True)

        for m in range(n_pass):
            chunks, N = pass_chunks(m)
            OT(lambda m=m: load_x(m))
            for l in range(2):
                PH(lambda l=l, ch=chunks, N=N: ffn(l, 0, ch, N))
                OT(lambda l=l, ch=chunks, N=N: layernorm(3 * l + 0, ch, N))
                PH(lambda l=l, m=m, ch=chunks, N=N: conv(l, m, ch, N))
                OT(lambda l=l, ch=chunks, N=N: layernorm(3 * l + 1, ch, N))
                PH(lambda l=l, ch=chunks, N=N: ffn(l, 1, ch, N))
                OT(lambda l=l, ch=chunks, N=N: layernorm(3 * l + 2, ch, N))
            PH(lambda m=m, ch=chunks, N=N: kvproj(m, ch, N))
            OT(lambda m=m: bounce_out(m))
            if do_phase2:
                OT(lambda m=m: collectives([m]))
        if do_phase2:
            for m in range(n_pass):
                chunks, N = pass_chunks(m)
                OT(lambda m=m: bounce_in(m))
                for l in range(2, 4):
                    PH(lambda l=l, ch=chunks, N=N: ffn(l, 0, ch, N))
                    OT(lambda l=l, ch=chunks, N=N: layernorm(3 * l + 0, ch, N))
                    PH(lambda l=l, m=m, ch=chunks, N=N: attn(l, m, ch, N))
                    OT(lambda l=l, ch=chunks, N=N: layernorm(3 * l + 1, ch, N))
                    PH(lambda l=l, ch=chunks, N=N: ffn(l, 1, ch, N))
                    OT(lambda l=l, ch=chunks, N=N: layernorm(3 * l + 2, ch, N, last=(l == 3)))
                OT(lambda m=m: store_y(m))

        gens = {}
        ph_idx = [i_ for i_, st_ in enumerate(steps) if st_[0] == "ph"]

        def begin(i_):
            g_ = steps[i_][1]()
            next(g_)
            gens[i_] = g_

        if ph_idx:
            begin(ph_idx[0])
        for i_, (kind_, f_) in enumerate(steps):
            if kind_ == "ot":
                f_()
            else:
                for _ in gens.pop(i_):
                    pass
                nxt_ = [j_ for j_ in ph_idx if j_ > i_]
                if nxt_:
                    begin(nxt_[0])
        P.emit()
    return nc


_NC_CACHE = {}


def _host_inputs(inp):
    f = np.float32
    xp = np.asarray(inp["x_prompt"], f); xs = np.asarray(inp["x_sample"], f)
    ck = np.asarray(inp["cache_k"], f).reshape(16, 2048, D); cv = np.asarray(inp["cache_v"], f).reshape(16, 2048, D)
    sc = np.asarray(inp["state_conv"], f)
    lng = np.asarray(inp["ln_g"], f).reshape(12, D); lnb = np.asarray(inp["ln_b"], f).reshape(12, D)
    wc = np.asarray(inp["w_conv"], f).reshape(6, D)
    shared = {k: np.ascontiguousarray(np.asarray(inp[k], f)) for k in
              ("w_ffn_gate", "w_ffn_up", "w_ffn_down", "w_conv_in", "w_conv_out", "w_kv", "w_q", "w_o")}
    smask = np.zeros((SM, 512), f)
    for s_ in range(2):
        for h in range(16):
            for t in range(16):
                for kk in range(t):
                    smask[A_COLS[s_] + kk, 256 * s_ + 16 * h + t] = 1.0
    maps = []
    sidx = np.arange(128)[:, None]; tidx = np.arange(512)[None, :]
    for c in range(8):
        b, j = divmod(c, 4)
        xin = np.zeros((NTOK, D), f)
        for m in range(4):
            t0 = 512 * (4 * m + j)
            xin[512 * m:512 * (m + 1)] = xp[b, t0:t0 + 512]
            if t0 > 0:
                xin[2048 + 4 * m:2048 + 4 * m + 4] = xp[b, t0 - 4:t0]
        for s_ in range(2):
            xin[2048 + A_COLS[s_]:2048 + A_COLS[s_] + 16] = xs[2 * c + s_]
        prm = np.concatenate([lng, lnb, wc, sc[:, 2 * c:2 * c + 2].reshape(8, D)], 0)
        masks = np.zeros((128, 16 * 512), f)
        for i in range(4):
            for kb in range(4):
                mi = i * 4 + kb
                masks[:, mi * 512:(mi + 1) * 512] = ((512 * i + 128 * kb + sidx) < (512 * j + tidx)).astype(f)
        hm = np.full((128, 1), 0.0 if j == 0 else 1.0, f)
        d = {"xin": xin, "prm": np.ascontiguousarray(prm), "masks": masks, "smask": smask, "hmask": hm,
             "cache_k": np.ascontiguousarray(ck[2 * c:2 * c + 2]), "cache_v": np.ascontiguousarray(cv[2 * c:2 * c + 2])}
        d.update(shared)
        maps.append(d)
    return maps


def _assemble(results):
    f = np.float32
    y_p = np.zeros((2, 8192, D), f); k_p = np.zeros((2, 8192, D), f); v_p = np.zeros((2, 8192, D), f)
    y_s = np.zeros((16, 16, D), f); k_s = np.zeros((16, 16, D), f); v_s = np.zeros((16, 16, D), f)
    conv_p = np.zeros((2, 2, 2, D), f); conv_s = np.zeros((2, 16, 2, D), f)
    for c in range(8):
        b, j = divmod(c, 4)
        rr = results[c]
        for m in range(4):
            t0 = 512 * (4 * m + j)
            y_p[b, t0:t0 + 512] = rr["y_out"][512 * m:512 * (m + 1)]
            k_p[b, t0:t0 + 512] = rr["k_out"][512 * m:512 * (m + 1)]
            v_p[b, t0:t0 + 512] = rr["v_out"][512 * m:512 * (m + 1)]
        for s_ in range(2):
            a0 = 2048 + A_COLS[s_]
            y_s[2 * c + s_] = rr["y_out"][a0:a0 + 16]
            k_s[2 * c + s_] = rr["k_out"][a0:a0 + 16]
            v_s[2 * c + s_] = rr["v_out"][a0:a0 + 16]
            conv_s[:, 2 * c + s_] = rr["conv_out"][:, 2 + 2 * s_:4 + 2 * s_]
        if j == 3:
            conv_p[:, b] = rr["conv_out"][:, 0:2]
    return (y_p, y_s, k_p.reshape(2, 8192, 16, 64), v_p.reshape(2, 8192, 16, 64), conv_p,
            k_s.reshape(16, 16, 16, 64), v_s.reshape(16, 16, 16, 64), conv_s)


def kernel(**inputs):
    if "nc" not in _NC_CACHE:
        _NC_CACHE["nc"] = build()
    nc = _NC_CACHE["nc"]
    maps = _host_inputs(inputs)
    res = run_bass_kernel_spmd(nc, maps, core_ids=list(range(8)))
    return _assemble(res.results)
```

```python
import functools
import numpy as np
import concourse.bass as bass
import concourse.mybir as mybir

F32 = mybir.dt.float32
BF16 = mybir.dt.bfloat16
I32 = mybir.dt.int32
AF = mybir.ActivationFunctionType
ALU = mybir.AluOpType

ENGS = ("pe", "act", "dve", "pool", "sp")
EPOCH = 30000


class Res:
    __slots__ = ("name", "w", "rd", "rd_dma")

    def __init__(self, name=""):
        self.name = name
        self.w = None
        self.rd = {}
        self.rd_dma = []


class Op:
    __slots__ = ("eng", "fn", "deps", "dma", "sig", "sigidx", "dsem", "dtarget", "dprev", "pos", "inc")

    def __init__(self, eng, fn, dma, inc=16):
        self.inc = inc
        self.eng = eng
        self.fn = fn
        self.dma = dma
        self.deps = []
        self.sig = False
        self.sigidx = None
        self.dsem = None
        self.dtarget = None
        self.dprev = None
        self.pos = None


class Prog:
    def __init__(self, nc, n_dsem=None):
        self.nc = nc
        self.ops = {e: [] for e in ENGS}
        self.n_dsem = n_dsem or {"sp": 24, "pool": 12, "act": 4, "cc": 2}
        self.dma_rr = {e: 0 for e in self.n_dsem}
        self.dma_last = {e: [None] * n for e, n in self.n_dsem.items()}
        self.dma_tgt = {e: [0] * n for e, n in self.n_dsem.items()}
        self.all_dma = []

    def op(self, eng, fn, reads=(), writes=(), dma=False, cc=False):
        o = Op(eng, fn, dma, 1 if cc else 16)

        def _flat(xs):
            out = []
            for x_ in xs:
                if isinstance(x_, (list, tuple)):
                    out.extend(_flat(x_))
                elif x_ is not None:
                    out.append(x_)
            return out
        reads = _flat(reads); writes = _flat(writes)
        deps = []
        for r in reads:
            if r.w is not None:
                deps.append(r.w)
        for w in writes:
            if w.w is not None:
                deps.append(w.w)
            deps.extend(w.rd.values())
            deps.extend(w.rd_dma)
        for r in reads:
            if dma:
                r.rd_dma.append(o)
            else:
                r.rd[eng] = o
        for w in writes:
            w.w = o
            w.rd = {}
            w.rd_dma = []
        seen = set()
        for d in deps:
            if d is o or id(d) in seen:
                continue
            seen.add(id(d))
            if (not d.dma) and (not dma) and d.eng == "pe" and eng == "pe":
                continue
            o.deps.append(d)
            if not d.dma:
                d.sig = True
        if dma:
            pl = "cc" if cc else eng
            k = self.dma_rr[pl]
            self.dma_rr[pl] = (k + 1) % self.n_dsem[pl]
            o.dsem = (pl, k)
            o.dprev = self.dma_tgt[pl][k]
            self.dma_tgt[pl][k] += o.inc
            o.dtarget = self.dma_tgt[pl][k]
            self.all_dma.append(o)
        o.pos = len(self.ops[eng])
        self.ops[eng].append(o)
        return o

    def mm(self, out, lhsT, rhs, start, stop, reads, writes, **kw):
        return self.op("pe", lambda e: e.matmul(out, lhsT, rhs, start=start, stop=stop, **kw), reads, writes)

    def tr(self, out, in_, ident, reads, writes):
        return self.op("pe", lambda e: e.transpose(out, in_, ident), reads, writes)

    def act(self, out, in_, func, reads, writes, eng="act", **kw):
        fname = getattr(func, "name", str(func))
        if fname in ("Copy", "Identity"):
            sc = kw.get("scale"); bi = kw.get("bias")
            if sc is None and bi is None:
                return self.op("dve", lambda e: e.tensor_copy(out, in_), reads, writes)
            if bi is None:
                return self.op("dve", lambda e: e.tensor_scalar(out, in_, sc, None, ALU.mult), reads, writes)
            sc2 = 1.0 if sc is None else sc
            return self.op("dve", lambda e: e.tensor_scalar(out, in_, sc2, bi, ALU.mult, ALU.add), reads, writes)
        if fname == "Square":
            return self.op("dve", lambda e: e.tensor_tensor(out, in_, in_, ALU.mult), reads, writes)
        return self.op(eng, lambda e: e.activation(out, in_, func, **kw), reads, writes)

    def tt(self, eng, out, in0, in1, op, reads, writes):
        return self.op(eng, lambda e: e.tensor_tensor(out, in0, in1, op), reads, writes)

    def ts(self, eng, out, in0, s1, s2, op0, op1, reads, writes):
        if op1 is None:
            return self.op(eng, lambda e: e.tensor_scalar(out, in0, s1, None, op0), reads, writes)
        return self.op(eng, lambda e: e.tensor_scalar(out, in0, s1, s2, op0, op1), reads, writes)

    def stt(self, eng, out, in0, scalar, in1, op0, op1, reads, writes):
        return self.op(eng, lambda e: e.scalar_tensor_tensor(out, in0, scalar, in1, op0, op1), reads, writes)

    def copy(self, eng, out, in_, reads, writes):
        if eng == "act":
            eng = "dve"
        return self.op(eng, lambda e: e.tensor_copy(out, in_), reads, writes)

    def memset(self, eng, ap, val, writes):
        return self.op(eng, lambda e: e.memset(ap, val), (), writes)

    def dma(self, q, out, in_, reads, writes, **kw):
        return self.op(q, lambda e: e.dma_start(out=out, in_=in_, **kw), reads, writes, dma=True)

    def emit(self):
        nc = self.nc
        import contextlib
        with contextlib.ExitStack() as st:
            csems = {}
            for e in ("pe", "act", "dve", "pool"):
                n = 0
                for o in self.ops[e]:
                    if o.sig:
                        o.sigidx = n
                        n += 1
                nep = max(1, (n + EPOCH - 1) // EPOCH)
                csems[e] = [st.enter_context(nc.semaphore(f"c_{e}_{i}")) for i in range(nep)]
            dsems = {}
            for q, n in self.n_dsem.items():
                dsems[q] = [st.enter_context(nc.semaphore(f"d_{q}_{i}")) for i in range(n)]
            block = st.enter_context(nc.Block())

            def run_engine(ename, eng):
                waited_c = {}
                waited_d = {}

                def wait_dep(d):
                    if d.dma:
                        key = d.dsem
                        if waited_d.get(key, 0) >= d.dtarget:
                            return
                        waited_d[key] = d.dtarget
                        eng.wait_ge(dsems[key[0]][key[1]], d.dtarget)
                    else:
                        if waited_c.get(d.eng, -1) >= d.sigidx:
                            return
                        waited_c[d.eng] = d.sigidx
                        ep, v = divmod(d.sigidx, EPOCH)
                        eng.wait_ge(csems[d.eng][ep], v + 1)

                for o in self.ops[ename]:
                    for d in o.deps:
                        wait_dep(d)
                    if o.dma:
                        q, k = o.dsem
                        if o.dprev > 0 and waited_d.get(o.dsem, 0) < o.dprev:
                            waited_d[o.dsem] = o.dprev
                            eng.wait_ge(dsems[q][k], o.dprev)
                        ins = o.fn(eng)
                        ins.then_inc(dsems[q][k], o.inc)
                    else:
                        ins = o.fn(eng)
                        if o.sig:
                            ep, v = divmod(o.sigidx, EPOCH)
                            ins.then_inc(csems[ename][ep], 1)
                if ename == "sp":
                    for q, n in self.n_dsem.items():
                        for k in range(n):
                            t = self.dma_tgt[q][k]
                            if t > 0 and waited_d.get((q, k), 0) < t:
                                eng.wait_ge(dsems[q][k], t)

            @block.tensor
            def _(e):
                run_engine("pe", e)

            @block.scalar
            def _(e):
                run_engine("act", e)

            @block.vector
            def _(e):
                run_engine("dve", e)

            @block.gpsimd
            def _(e):
                run_engine("pool", e)

            @block.sync
            def _(e):
                run_engine("sp", e)


import contextlib
from concourse.bass_utils import run_bass_kernel_spmd

D = 1024
NCH = 8
FF = 2816
NF = 22
MT = 512
SM = 52
NTOK = 2048 + SM
ALPHA = 8.0 ** 0.25
EPS = 1e-5
A_COLS = (18, 36)


def build(n_pass=4, do_phase2=True, stage=99):
    nc = bass.Bass("TRN2", target_bir_lowering=False)

    import os as _os
    TINY = _os.environ.get("KTINY", "") == "1"

    def din(name, shape, dt=F32):
        if TINY and (name.startswith("w_") or name.startswith("cache")):
            shape = [2] * len(shape)
        return nc.dram_tensor(name, shape, dt, kind="ExternalInput").ap()

    def dout(name, shape, dt=F32):
        return nc.dram_tensor(name, shape, dt, kind="ExternalOutput").ap()

    def dint(name, shape, dt):
        return nc.dram_tensor(name, shape, dt).ap()

    xin = din("xin", [NTOK, D])
    prm = din("prm", [38, D])
    masks_d = din("masks", [128, 16 * 512])
    smask_d = din("smask", [SM, 512])
    hmask_d = din("hmask", [128, 1])
    ck_d = din("cache_k", [2, 2048, D])
    cv_d = din("cache_v", [2, 2048, D])
    w_gate = din("w_ffn_gate", [4, 2, D, FF])
    w_up = din("w_ffn_up", [4, 2, D, FF])
    w_down = din("w_ffn_down", [4, 2, FF, D])
    w_cin = din("w_conv_in", [2, D, 3 * D])
    w_cout = din("w_conv_out", [2, D, D])
    w_kv = din("w_kv", [D, 2 * D])
    w_q = din("w_q", [2, D, D])
    w_o = din("w_o", [2, D, D])
    y_out = dout("y_out", [NTOK, D])
    k_out = dout("k_out", [NTOK, D])
    v_out = dout("v_out", [NTOK, D])
    conv_out = dout("conv_out", [2, 6, D])
    kT_in = [dint(f"kT_in{m_}", [D, MT], BF16) for m_ in range(4)]
    v_in = [dint(f"v_in{m_}", [MT, D], BF16) for m_ in range(4)]
    kT_all = [dint(f"kT_all{m_}", [4 * D, MT], BF16) for m_ in range(4)]
    v_all = [dint(f"v_all{m_}", [4 * MT, D], BF16) for m_ in range(4)]
    xa_d = dint("xa_d", [128, NCH * NTOK], F32)
    xb_d = dint("xb_d", [128, NCH * NTOK], BF16)

    with contextlib.ExitStack() as st:
        def sb(name, shape, dt):
            return st.enter_context(nc.sbuf_tensor(name, shape, dt))

        NMAX = 576
        P = Prog(nc)
        ps = [st.enter_context(nc.psum_tensor(f"ps{i}", [128, 512], F32)) for i in range(8)]
        Rps = [Res(f"ps{i}") for i in range(8)]

        def v3(t, n):
            return t[:, 0:NCH * n].rearrange("p (c n) -> p c n", c=NCH)

        xa_t = sb("xa", [128, NCH * NMAX], F32); xa = v3(xa_t, NMAX); Rxa = [Res(f"xa{c_}") for c_ in range(NCH)]
        xb_t = sb("xb", [128, NCH * NMAX], BF16); xb = v3(xb_t, NMAX); Rxb = Res("xb")
        r = xa; Rr = Rxa; rb = xb; Rrb = Rxb
        rq_t = sb("rq", [128, NCH * NMAX], BF16); rq = v3(rq_t, NMAX); Rrq = Res("rq")
        mean_s = sb("mean_s", [128, NMAX], F32); Rmean = Res("mean")
        rstd_s = sb("rstd_s", [128, NMAX], F32); Rrstd = Res("rstd")
        m2_s = sb("m2_s", [128, NMAX], F32); Rm2 = Res("m2")
        ident = sb("ident", [128, 128], F32); Rid = Res("ident")
        uneg = sb("uneg", [128, 128], BF16); ubar = sb("ubar", [128, 128], BF16)
        nones = sb("nones", [128, 128], BF16); meanm = sb("meanm", [128, 128], BF16)
        Rconst = Res("const")
        par_t = sb("par", [128, NCH * 38], F32); par = par_t[:, :].rearrange("p (c n) -> p c n", c=NCH); Rpar = Res("par")
        apar_t = sb("apar", [128, NCH * 24], F32); apar = apar_t[:, :].rearrange("p (c n) -> p c n", c=NCH)
        hmask = sb("hmask_sb", [128, 1], F32)
        masks = sb("masks_sb", [128, 16 * 512], BF16); Rmask = Res("masks")
        smask = sb("smask_sb", [SM, 512], BF16)
        stg = [sb(f"stg{i}", [128, D], F32) for i in range(2)]; Rstg = [Res("stg0"), Res("stg1")]
        ubm2 = [sb(f"ubm{i}", [128, 514], F32) for i in range(2)]; Rubm2 = [Res("ubm0"), Res("ubm1")]
        ulast_t = sb("ulast", [128, NCH * 32], F32); ulast = v3(ulast_t, 32); Rulast = Res("ulast")
        ubs_t = sb("ubs", [128, NCH * 96], F32); ubs = v3(ubs_t, 96); Rubs = Res("ubs")
        carry_t = sb("carry", [128, 2 * NCH * 8], F32)
        carry = carry_t[:, :].rearrange("p (l c m t) -> p l c m t", l=2, c=NCH, m=4); Rcarry = Res("carry")
        tmpf = [sb(f"tmpf{i}", [128, 512], F32) for i in range(2)]; Rtmpf = [Res("tf0"), Res("tf1")]
        ctmp = [sb(f"ctmp{i}", [128, 512], F32) for i in range(2)]; Rctmp = [Res("ct0"), Res("ct1")]
        kTs_t = sb("kTs", [128, NCH * SM], BF16); kTs = v3(kTs_t, SM); RkTs = Res("kTs")
        vs = sb("vs", [SM, D], BF16); Rvs = Res("vs")
        cst = sb("cst", [128, 16 * 128], F32); Rcst2 = [Res("cst0"), Res("cst1")]
        AR = sb("arena", [128, 52224], BF16)

        arena_res = []

        def arena_switch(layout):
            old_ops = []
            for rr in arena_res:
                if rr.w is not None:
                    old_ops.append(rr.w)
                old_ops.extend(rr.rd.values())
                old_ops.extend(rr.rd_dma)
            latest = {}
            dmas = []
            seen = set()
            for o in old_ops:
                if id(o) in seen:
                    continue
                seen.add(id(o))
                if o.dma:
                    dmas.append(o)
                else:
                    if o.eng not in latest or latest[o.eng].pos < o.pos:
                        latest[o.eng] = o
            fence = list(latest.values()) + dmas
            del arena_res[:]
            aps, res = {}, {}
            off = 0
            for name, n in layout:
                aps[name] = AR[:, off:off + n]
                rr = Res(name)
                rr.rd_dma = list(fence)
                res[name] = rr
                arena_res.append(rr)
                off += n
            assert off <= 52224, off
            return aps, res

        psc = {"i": 0}

        P.memset("pool", stg[0][:], 0.0, [Rstg[0]])
        P.memset("pool", stg[1][:], 0.0, [Rstg[1]])
        P.memset("pool", xa_t[:], 0.0, [Rxa])
        P.memset("pool", xb_t[:], 0.0, [Rxb])
        P.memset("pool", ubs_t[:], 0.0, [Rubs])
        P.memset("pool", ulast_t[:], 0.0, [Rulast])
        P.memset("pool", ident[:], 0.0, [Rid])
        P.op("pool", lambda e: e.affine_select(ident[:], ident[:], [[-1, 128]], ALU.not_equal, 1.0, base=0, channel_multiplier=1), [Rid], [Rid])
        import os
        SK = os.environ.get("KSKIP", "").split(",")
        if "consts" not in SK:
            P.memset("pool", uneg[:], -1.0, [Rconst])
            P.op("pool", lambda e: e.affine_select(uneg[:], uneg[:], [[-1, 128]], ALU.is_ge, 0.0, base=0, channel_multiplier=1), [Rconst], [Rconst])
            P.memset("pool", nones[:], -1.0, [Rconst])
            P.tt("pool", ubar[:], nones[:], uneg[:], ALU.subtract, [Rconst], [Rconst])
            P.memset("pool", meanm[:], 1.0 / D, [Rconst])
        if "masks" not in SK:
            for mi_ in range(16):
                P.dma("pool", masks[:, mi_ * 512:(mi_ + 1) * 512], masks_d[:, mi_ * 512:(mi_ + 1) * 512], [], [Rmask])
        if "smask" not in SK:
            P.dma("pool", smask[:], smask_d[:, :], [], [Rmask])
        if "hmask" not in SK:
            P.dma("sp", hmask[:], hmask_d[:, :], [], [Rpar])

        def load_tok(rows_ap, T, dst3, t0, Rdst, scale_dst=None):
            k = psc["i"] % 2; psc["i"] += 1
            Tp = ((T + 31) // 32) * 32
            P.dma("sp", stg[k][0:T, :], rows_ap, [], [Rstg[k]])
            for half in range(2):
                bank = 6 + half
                for cc in range(4):
                    c = half * 4 + cc
                    P.tr(ps[bank][:, cc * 128:cc * 128 + Tp], stg[k][0:Tp, c * 128:(c + 1) * 128], ident[0:Tp, 0:Tp],
                         [Rstg[k], Rid], [Rps[bank]])
                src = ps[bank][:, :].rearrange("p (c n) -> p c n", c=4)[:, :, 0:T]
                for (d3, Rd, eng, sc) in dst3:
                    dd = d3[:, half * 4:half * 4 + 4, t0:t0 + T]
                    if eng == "act":
                        if sc is None:
                            P.act(dd, src, AF.Copy, [Rps[bank]], [Rd])
                        else:
                            P.act(dd, src, AF.Identity, [Rps[bank]], [Rd], scale=sc)
                    else:
                        if sc is None:
                            P.copy(eng, dd, src, [Rps[bank]], [Rd])
                        else:
                            P.ts(eng, dd, src, sc, None, ALU.mult, None, [Rps[bank]], [Rd])

        def store_tok(src3, t0, T, rows_ap, Rsrc):
            k = psc["i"] % 2; psc["i"] += 1
            Tp = ((T + 31) // 32) * 32
            for half in range(2):
                bank = 6 + half
                for cc in range(4):
                    c = half * 4 + cc
                    P.tr(ps[bank][0:Tp, cc * 128:(cc + 1) * 128], src3[:, c, t0:t0 + Tp], ident[:, :],
                         [Rsrc, Rid], [Rps[bank]])
                if half == 0:
                    P.copy("dve", stg[k][0:T, 0:512], ps[bank][0:T, :], [Rps[bank]], [Rstg[k]])
                else:
                    P.act(stg[k][0:T, 512:1024], ps[bank][0:T, :], AF.Copy, [Rps[bank]], [Rstg[k]])
            P.dma("sp", rows_ap, stg[k][0:T, :], [Rstg[k]], [Res()])

        if "params" not in SK:
            load_tok(prm[:, :], 38, [(par, Rpar, "dve", None)], 0, Rpar)
            P.ts("dve", apar[:, :, :], par[:, :, 0:24], ALPHA, None, ALU.mult, None, [Rpar], [Rpar])

        def gcol(c, idx):
            return par[:, c, idx:idx + 1]

        def ln_prep(dm, n0, n):
            P.copy("dve", xb[:, dm, n0:n0 + n], xa[:, dm, n0:n0 + n], [Rxa[dm]], [Rxb])
            P.tt("dve", rq[:, dm, n0:n0 + n], xa[:, dm, n0:n0 + n], xa[:, dm, n0:n0 + n], ALU.mult, [Rxa[dm]], [Rrq])

        def layernorm(lidx, chunks, N, last=False):
            for (n0, n) in chunks:
                for c in range(NCH):
                    P.mm(ps[6][:, 0:n], meanm[:], rb[:, c, n0:n0 + n], c == 0, c == NCH - 1, [Rconst, Rrb], [Rps[6]])
                for c in range(NCH):
                    P.mm(ps[7][:, 0:n], meanm[:], rq[:, c, n0:n0 + n], c == 0, c == NCH - 1, [Rconst, Rrq], [Rps[7]])
                P.act(mean_s[:, n0:n0 + n], ps[6][:, 0:n], AF.Copy, [Rps[6]], [Rmean])
                P.tt("dve", m2_s[:, n0:n0 + n], mean_s[:, n0:n0 + n], mean_s[:, n0:n0 + n], ALU.mult, [Rmean], [Rm2])
                P.tt("dve", m2_s[:, n0:n0 + n], ps[7][:, 0:n], m2_s[:, n0:n0 + n], ALU.subtract, [Rps[7], Rm2], [Rm2])
                P.ts("dve", m2_s[:, n0:n0 + n], m2_s[:, n0:n0 + n], 0.0, EPS, ALU.max, ALU.add, [Rm2], [Rm2])
                P.act(rstd_s[:, n0:n0 + n], m2_s[:, n0:n0 + n], AF.Ln, [Rm2], [Rrstd])
                P.act(rstd_s[:, n0:n0 + n], rstd_s[:, n0:n0 + n], AF.Exp, [Rrstd], [Rrstd], scale=-0.5)
            for c in range(NCH):
                P.tt("dve", r[:, c, 0:N], r[:, c, 0:N], mean_s[:, 0:N], ALU.subtract, [Rr[c], Rmean], [Rr[c]])
                P.tt("dve", r[:, c, 0:N], r[:, c, 0:N], rstd_s[:, 0:N], ALU.mult, [Rr[c], Rrstd], [Rr[c]])
                P.act(xb[:, c, 0:N], r[:, c, 0:N], AF.Identity, [Rr[c], Rpar], [Rxb],
                      scale=par[:, c, lidx:lidx + 1], bias=par[:, c, 12 + lidx:13 + lidx])
                if last:
                    P.ts("pool", xa[:, c, 0:N], r[:, c, 0:N], par[:, c, lidx:lidx + 1], par[:, c, 12 + lidx:13 + lidx],
                         ALU.mult, ALU.add, [Rr[c], Rpar], [Rxa[c]])
                else:
                    P.ts("pool", xa[:, c, 0:N], r[:, c, 0:N], apar[:, c, lidx:lidx + 1], apar[:, c, 12 + lidx:13 + lidx],
                         ALU.mult, ALU.add, [Rr[c], Rpar], [Rxa[c]])

        def ffn(l, i, chunks, N):
            NBG, NBD = 4, 3
            aps, res = arena_switch([(f"wg{i_}", 2048) for i_ in range(NBG)] + [(f"wu{i_}", 2048) for i_ in range(NBG)] +
                                    [(f"wd{i_}", 5632) for i_ in range(NBD)] + [("hT", NF * NMAX)])
            wg = [aps[f"wg{i_}"].rearrange("p (c f) -> p c f", c=NCH) for i_ in range(NBG)]
            wu = [aps[f"wu{i_}"].rearrange("p (c f) -> p c f", c=NCH) for i_ in range(NBG)]
            wd = [aps[f"wd{i_}"].rearrange("p (f d) -> p f d", f=NF) for i_ in range(NBD)]
            hT = aps["hT"].rearrange("p (f n) -> p f n", f=NF)
            Rwg = [res[f"wg{i_}"] for i_ in range(NBG)]; Rwu = [res[f"wu{i_}"] for i_ in range(NBG)]
            Rwd = [res[f"wd{i_}"] for i_ in range(NBD)]; RhT = res["hT"]
            WG = w_gate[l, i].rearrange("(c p) f -> p c f", p=128)
            WU = w_up[l, i].rearrange("(c p) f -> p c f", p=128)
            WD = w_down[l, i].rearrange("(f p) d -> p f d", p=128)

            def load_gu(s):
                b = s % NBG
                P.dma("pool", wg[b], WG[:, :, s * 256:(s + 1) * 256], [], [Rwg[b]])
                P.dma("pool", wu[b], WU[:, :, s * 256:(s + 1) * 256], [], [Rwu[b]])

            def load_d(s):
                b = s % NBD
                P.dma("pool", wd[b], WD[:, :, s * 256:(s + 1) * 256], [], [Rwd[b]])

            for s_ in range(NBG):
                load_gu(s_)
            yield
            k = 0
            for s in range(11):
                b = s % NBG
                for fc in range(2):
                    f = 2 * s + fc
                    for (n0, n) in chunks:
                        gbk = k % 2; ubk = 2 + k % 2; k += 1
                        for c in range(NCH):
                            P.mm(ps[gbk][:, 0:n], wg[b][:, c, fc * 128:(fc + 1) * 128], xb[:, c, n0:n0 + n],
                                 c == 0, c == NCH - 1, [Rwg[b], Rxb], [Rps[gbk]])
                        for c in range(NCH):
                            P.mm(ps[ubk][:, 0:n], wu[b][:, c, fc * 128:(fc + 1) * 128], xb[:, c, n0:n0 + n],
                                 c == 0, c == NCH - 1, [Rwu[b], Rxb], [Rps[ubk]])
                        tf = k % 2
                        P.act(tmpf[tf][:, 0:n], ps[gbk][:, 0:n], AF.Silu, [Rps[gbk]], [Rtmpf[tf]])
                        P.tt("dve", hT[:, f, n0:n0 + n], tmpf[tf][:, 0:n], ps[ubk][:, 0:n], ALU.mult,
                             [Rtmpf[tf], Rps[ubk]], [RhT])
                if s + NBG < 11:
                    load_gu(s + NBG)
                if s < NBD:
                    load_d(s)
            for s in range(4):
                b = s % NBD
                for dc in range(2):
                    dm = 2 * s + dc
                    for (n0, n) in chunks:
                        yb = 4 + k % 2; k += 1
                        for f in range(NF):
                            P.mm(ps[yb][:, 0:n], wd[b][:, f, dc * 128:(dc + 1) * 128], hT[:, f, n0:n0 + n],
                                 f == 0, f == NF - 1, [Rwd[b], RhT], [Rps[yb]])
                        P.stt("dve", r[:, dm, n0:n0 + n], ps[yb][:, 0:n], 0.5, xa[:, dm, n0:n0 + n], ALU.mult, ALU.add,
                              [Rps[yb], Rxa[dm]], [Rr[dm]])
                        ln_prep(dm, n0, n)
                if s + NBD < 4:
                    load_d(s + NBD)

        def conv(l, m, chunks, N):
            aps, res = arena_switch([("win0", 3072), ("win1", 3072), ("gT", NCH * NMAX), ("wo0", 2048), ("wo1", 2048)])
            win = [aps["win0"].rearrange("p (c g f) -> p c g f", c=NCH, g=3), aps["win1"].rearrange("p (c g f) -> p c g f", c=NCH, g=3)]
            gT = aps["gT"].rearrange("p (c n) -> p c n", c=NCH)
            wo = [aps["wo0"].rearrange("p (c f) -> p c f", c=NCH), aps["wo1"].rearrange("p (c f) -> p c f", c=NCH)]
            Rwin = [res["win0"], res["win1"]]; RgT = res["gT"]; Rwo = [res["wo0"], res["wo1"]]
            WI = w_cin[l].rearrange("(c p) (g i f) -> p c g i f", p=128, g=3, i=NCH)
            WO = w_cout[l].rearrange("(c p) d -> p c d", p=128)

            def load_in(i):
                b = i % 2
                for g in range(3):
                    P.dma("pool", win[b][:, :, g, :], WI[:, :, g, i, :], [], [Rwin[b]])

            def load_o(s):
                P.dma("pool", wo[s % 2], WO[:, :, s * 256:(s + 1) * 256], [], [Rwo[s % 2]])

            load_in(0)
            load_in(1)
            load_o(0)
            load_o(1)
            yield
            order = list(reversed(chunks))
            if m == 0:
                P.memset("pool", gT[:, :, MT:MT + 2], 0.0, [RgT])
            k = 0
            for i in range(NCH):
                b = i % 2
                w0 = par[:, i, 24 + 3 * l:25 + 3 * l]; w1 = par[:, i, 25 + 3 * l:26 + 3 * l]; w2 = par[:, i, 26 + 3 * l:27 + 3 * l]
                for (n0, n) in order:
                    small = n0 >= MT
                    base = 3 * (k % 2); k += 1
                    for g in range(3):
                        bank = base + g
                        for c in range(NCH):
                            P.mm(ps[bank][:, 0:n], win[b][:, c, g, :], xb[:, c, n0:n0 + n], c == 0, c == NCH - 1,
                                 [Rwin[b], Rxb], [Rps[bank]])
                    tf = k % 2
                    P.act(tmpf[tf][:, 0:n], ps[base + 1][:, 0:n], AF.Copy, [Rps[base + 1]], [Rtmpf[tf]])
                    ct = ctmp[tf]; Rct = Rctmp[tf]
                    if small:
                        P.tt("dve", ubs[:, i, 0:SM], tmpf[tf][:, 0:n], ps[base + 2][:, 0:n], ALU.mult,
                             [Rtmpf[tf], Rps[base + 2]], [Rubs])
                        for s_ in range(2):
                            pc = A_COLS[s_] - 2
                            P.copy("pool", ubs[:, i, pc:pc + 2], par[:, i, 30 + 4 * l + 2 * s_:32 + 4 * l + 2 * s_], [Rpar], [Rubs])
                        nn = SM - 2
                        P.ts("dve", ct[:, 0:nn], ubs[:, i, 0:nn], w0, None, ALU.mult, None, [Rubs, Rpar], [Rct])
                        P.stt("dve", ct[:, 0:nn], ubs[:, i, 1:nn + 1], w1, ct[:, 0:nn], ALU.mult, ALU.add, [Rubs, Rpar, Rct], [Rct])
                        P.stt("dve", ct[:, 0:nn], ubs[:, i, 2:nn + 2], w2, ct[:, 0:nn], ALU.mult, ALU.add, [Rubs, Rpar, Rct], [Rct])
                        P.tt("dve", gT[:, i, MT + 2:MT + SM], ct[:, 0:nn], ps[base][:, 2:SM], ALU.mult, [Rct, Rps[base]], [RgT])
                        for mm_ in range(4):
                            if mm_ == 0:
                                P.ts("pool", carry[:, l, i, 0, :], ubs[:, i, 2:4], hmask[:, 0:1], None, ALU.mult, None, [Rubs, Rpar], [Rcarry])
                            else:
                                P.copy("pool", carry[:, l, i, mm_, :], ubs[:, i, 4 * mm_ + 2:4 * mm_ + 4], [Rubs], [Rcarry])
                    else:
                        ubm = ubm2[i % 2]; Rubm = Rubm2[i % 2]
                        P.copy("pool", ubm[:, 0:2], carry[:, l, i, m, :], [Rcarry], [Rubm])
                        P.tt("dve", ubm[:, 2:514], tmpf[tf][:, 0:n], ps[base + 2][:, 0:n], ALU.mult,
                             [Rtmpf[tf], Rps[base + 2]], [Rubm])
                        P.copy("pool", ulast[:, i, 0:2], ubm[:, 512:514], [Rubm], [Rulast])
                        P.ts("dve", ct[:, 0:n], ubm[:, 0:n], w0, None, ALU.mult, None, [Rubm, Rpar], [Rct])
                        P.stt("dve", ct[:, 0:n], ubm[:, 1:n + 1], w1, ct[:, 0:n], ALU.mult, ALU.add, [Rubm, Rpar, Rct], [Rct])
                        P.stt("dve", ct[:, 0:n], ubm[:, 2:n + 2], w2, ct[:, 0:n], ALU.mult, ALU.add, [Rubm, Rpar, Rct], [Rct])
                        P.tt("dve", gT[:, i, 0:n], ct[:, 0:n], ps[base][:, 0:n], ALU.mult, [Rct, Rps[base]], [RgT])
                if i + 2 < NCH:
                    load_in(i + 2)
            for s in range(4):
                b = s % 2
                for dc in range(2):
                    dm = 2 * s + dc
                    for (n0, n) in chunks:
                        yb = 6 + k % 2; k += 1
                        for c in range(NCH):
                            P.mm(ps[yb][:, 0:n], wo[b][:, c, dc * 128:(dc + 1) * 128], gT[:, c, n0:n0 + n], c == 0, c == NCH - 1,
                                 [Rwo[b], RgT], [Rps[yb]])
                        P.tt("dve", r[:, dm, n0:n0 + n], ps[yb][:, 0:n], xa[:, dm, n0:n0 + n], ALU.add, [Rps[yb], Rxa[dm]], [Rr[dm]])
                        ln_prep(dm, n0, n)
                if s + 2 < 4:
                    load_o(s + 2)
            if m == 3:
                store_tok(ulast, 0, 2, conv_out[l, 0:2, :], Rulast)
            if m == 0:
                store_tok(ubs, A_COLS[0] + 14, 2, conv_out[l, 2:4, :], Rubs)
                store_tok(ubs, A_COLS[1] + 14, 2, conv_out[l, 4:6, :], Rubs)

        RkTin = [Res(f"kTin{m}") for m in range(4)]
        Rvin = [Res(f"vin{m}") for m in range(4)]

        def kvproj(m, chunks, N):
            aps, res = arena_switch([("wk0", 2048), ("wk1", 2048), ("wt0", 4096), ("wt1", 4096), ("ktl", NCH * NMAX),
                                     ("vb0", 512), ("vb1", 512)])
            wk = [aps["wk0"].rearrange("p (c f) -> p c f", c=NCH), aps["wk1"].rearrange("p (c f) -> p c f", c=NCH)]
            wt = [aps["wt0"].rearrange("p (c f) -> p c f", c=NCH), aps["wt1"].rearrange("p (c f) -> p c f", c=NCH)]
            ktl = aps["ktl"].rearrange("p (c n) -> p c n", c=NCH)
            vb = [aps["vb0"], aps["vb1"]]
            Rwk = [res["wk0"], res["wk1"]]; Rwt = [res["wt0"], res["wt1"]]; Rktl = res["ktl"]; Rvb = [res["vb0"], res["vb1"]]
            WKV = w_kv.rearrange("(c p) d -> p c d", p=128)
            P.dma("pool", wk[0], WKV[:, :, 0:256], [], [Rwk[0]])
            P.dma("pool", wk[1], WKV[:, :, 256:512], [], [Rwk[1]])
            P.dma("pool", wt[0], WKV[:, :, 0:512], [], [Rwt[0]])
            P.dma("pool", wt[1], WKV[:, :, 512:1024], [], [Rwt[1]])
            yield
            k = 0
            for s in range(4):
                b = s % 2
                for dc in range(2):
                    dm = 2 * s + dc
                    for (n0, n) in chunks:
                        yb = 4 + k % 2; k += 1
                        for c in range(NCH):
                            P.mm(ps[yb][:, 0:n], wk[b][:, c, dc * 128:(dc + 1) * 128], xb[:, c, n0:n0 + n], c == 0, c == NCH - 1,
                                 [Rwk[b], Rxb], [Rps[yb]])
                        P.act(ktl[:, dm, n0:n0 + n], ps[yb][:, 0:n], AF.Copy, [Rps[yb]], [Rktl])
                if s + 2 < 4:
                    P.dma("pool", wk[b], WKV[:, :, (s + 2) * 256:(s + 3) * 256], [], [Rwk[b]])
            P.dma("sp", kT_in[m].rearrange("(c p) t -> p c t", p=128), ktl[:, :, 0:MT], [Rktl], [RkTin[m]])
            if m == 0:
                P.copy("pool", kTs[:, :, :], ktl[:, :, MT:MT + SM], [Rktl], [RkTs])
            blocks = [(tb * 128, 128) for tb in range(4)] + ([(MT, SM)] if m == 0 else [])
            for g in range(4):
                b = g % 2
                outd = k_out if g < 2 else v_out
                cs = (g % 2) * 512
                for (t0, T) in blocks:
                    bank = k % 2; k += 1
                    for c in range(NCH):
                        P.mm(ps[bank][0:T, :], xb[:, c, t0:t0 + T], wt[b][:, c, :], c == 0, c == NCH - 1, [Rxb, Rwt[b]], [Rps[bank]])
                    tf = k % 2
                    P.act(tmpf[tf][0:T, :], ps[bank][0:T, :], AF.Copy, [Rps[bank]], [Rtmpf[tf]])
                    row0 = (MT * m + t0) if t0 < MT else (2048 + t0 - MT)
                    P.dma("sp", outd[row0:row0 + T, cs:cs + 512], tmpf[tf][0:T, :], [Rtmpf[tf]], [Res()])
                    if g >= 2:
                        if t0 < MT:
                            P.copy("dve", vb[tf][0:T, :], ps[bank][0:T, :], [Rps[bank]], [Rvb[tf]])
                            P.dma("sp", v_in[m][t0:t0 + T, cs:cs + 512], vb[tf][0:T, :], [Rvb[tf]], [Rvin[m]])
                        else:
                            P.copy("dve", vs[0:T, cs:cs + 512], ps[bank][0:T, :], [Rps[bank]], [Rvs])
                if g + 2 < 4:
                    P.dma("pool", wt[b], WKV[:, :, (g + 2) * 512:(g + 3) * 512], [], [Rwt[b]])

        RkTall = [Res(f"kTall{m_}") for m_ in range(4)]; Rvall = [Res(f"vall{m_}") for m_ in range(4)]

        def attn(l, m, chunks, N):
            li = l - 2
            aps, res = arena_switch([("qT", NCH * NMAX), ("oT", NCH * NMAX), ("kT0", 8192), ("kT1", 8192), ("vv0", 8192), ("vv1", 8192),
                                     ("e00", 512), ("e01", 512), ("e10", 512), ("e11", 512),
                                     ("sp00", 512), ("sp01", 512), ("sp10", 512), ("sp11", 512),
                                     ("ex0", 512), ("ex1", 512), ("ww0", 512), ("ww1", 512), ("wq0", 2048), ("wq1", 2048)])
            qT = aps["qT"].rearrange("p (c n) -> p c n", c=NCH); RqT = res["qT"]
            oT = aps["oT"].rearrange("p (c n) -> p c n", c=NCH); RoT = res["oT"]
            kTb = [aps["kT0"], aps["kT1"]]; RkTb = [res["kT0"], res["kT1"]]
            vvb = [aps["vv0"].rearrange("p (b d) -> p b d", d=128), aps["vv1"].rearrange("p (b d) -> p b d", d=128)]
            Rvvb = [res["vv0"], res["vv1"]]
            eb = [[aps["e00"], aps["e01"]], [aps["e10"], aps["e11"]]]
            Reb = [[res["e00"], res["e01"]], [res["e10"], res["e11"]]]
            spb = [[aps["sp00"], aps["sp01"]], [aps["sp10"], aps["sp11"]]]
            Rspb = [[res["sp00"], res["sp01"]], [res["sp10"], res["sp11"]]]
            exb = [aps["ex0"], aps["ex1"]]; Rexb = [res["ex0"], res["ex1"]]
            wwb = [aps["ww0"], aps["ww1"]]; Rwwb = [res["ww0"], res["ww1"]]
            wq = [aps["wq0"].rearrange("p (c f) -> p c f", c=NCH), aps["wq1"].rearrange("p (c f) -> p c f", c=NCH)]
            Rwq = [res["wq0"], res["wq1"]]
            WQ = w_q[li].rearrange("(c p) d -> p c d", p=128)
            WO = w_o[li].rearrange("(c p) d -> p c d", p=128)
            P.dma("pool", wq[0], WQ[:, :, 0:256], [], [Rwq[0]])
            P.dma("pool", wq[1], WQ[:, :, 256:512], [], [Rwq[1]])
            yield
            k = 0
            for s in range(4):
                b = s % 2
                for dc in range(2):
                    dm = 2 * s + dc
                    for (n0, n) in chunks:
                        yb = 6 + k % 2; k += 1
                        for c in range(NCH):
                            P.mm(ps[yb][:, 0:n], wq[b][:, c, dc * 128:(dc + 1) * 128], xb[:, c, n0:n0 + n], c == 0, c == NCH - 1,
                                 [Rwq[b], Rxb], [Rps[yb]])
                        P.act(qT[:, dm, n0:n0 + n], ps[yb][:, 0:n], AF.Identity, [Rps[yb]], [RqT], scale=0.125)
                if s + 2 < 4:
                    P.dma("pool", wq[b], WQ[:, :, (s + 2) * 256:(s + 3) * 256], [], [Rwq[b]])
                else:
                    P.dma("pool", wq[b], WO[:, :, (s - 2) * 256:(s - 1) * 256], [], [Rwq[b]])
            if m == 0:
                P.memset("pool", oT[:, :, MT:MT + SM], 0.0, [RoT])

            def emit_chains(blocks, q_aps, nq):
                nb = len(blocks)

                def zmm(a):
                    blk = blocks[a]; K = blk["K"]; pr = a % 2
                    for hh in range(2):
                        Z = 2 * hh + pr
                        P.mm(ps[Z][0:K, 0:nq], blk["kT"][hh], q_aps[hh], True, True, [blk["Rk"], RqT], [Rps[Z]])

                zmm(0)
                for t in range(nb + 1):
                    b = t - 1
                    if b >= 0:
                        bb = blocks[b]; Kb = bb["K"]; pb = b % 2
                        for hh in range(2):
                            X = 4 + hh
                            P.mm(ps[X][0:Kb, 0:nq], uneg[0:Kb, 0:Kb], spb[hh][pb][0:Kb, 0:nq], b == 0, True,
                                 [Rconst, Rspb[hh][pb]], [Rps[X]], skip_group_check=True)
                    if t + 1 < nb:
                        zmm(t + 1)
                    if t < nb:
                        blk = blocks[t]; K = blk["K"]; pr = t % 2
                        for hh in range(2):
                            Z = 2 * hh + pr
                            P.act(eb[hh][pr][0:K, 0:nq], ps[Z][0:K, 0:nq], AF.Exp, [Rps[Z]], [Reb[hh][pr]])
                        if blk["mask"] is not None:
                            for hh in range(2):
                                P.tt("pool", eb[hh][pr][0:K, 0:nq], eb[hh][pr][0:K, 0:nq], blk["mask"], ALU.mult,
                                     [Reb[hh][pr], Rmask], [Reb[hh][pr]])
                    if b >= 0:
                        for hh in range(2):
                            X = 4 + hh
                            P.act(exb[hh][0:Kb, 0:nq], ps[X][0:Kb, 0:nq], AF.Exp, [Rps[X]], [Rexb[hh]])
                    if t < nb:
                        for hh in range(2):
                            P.act(spb[hh][pr][0:K, 0:nq], eb[hh][pr][0:K, 0:nq], AF.Ln, [Reb[hh][pr]], [Rspb[hh][pr]], bias=1.0)
                    if b >= 0:
                        if b < nb - 1:
                            for hh in range(2):
                                X = 4 + hh
                                if bb["restart52"]:
                                    P.mm(ps[X][:, 0:nq], nones[0:Kb, :], spb[hh][pb][0:Kb, 0:nq], True, True,
                                         [Rconst, Rspb[hh][pb]], [Rps[X]], skip_group_check=True)
                                else:
                                    P.mm(ps[X][:, 0:nq], ubar[:, :], spb[hh][pb][:, 0:nq], False, True,
                                         [Rconst, Rspb[hh][pb]], [Rps[X]], skip_group_check=True)
                        for hh in range(2):
                            P.tt("dve", wwb[hh][0:Kb, 0:nq], eb[hh][pb][0:Kb, 0:nq], exb[hh][0:Kb, 0:nq], ALU.mult,
                                 [Reb[hh][pb], Rexb[hh]], [Rwwb[hh]])
                        for hh in range(2):
                            O = 6 + hh
                            P.mm(ps[O][:, 0:nq], bb["v"], wwb[hh][0:Kb, 0:nq], b == 0, b == nb - 1,
                                 [bb["Rv"], Rwwb[hh]], [Rps[O]], skip_group_check=True)

            nkt = 4 * m + 4

            def load_kv(c):
                bf = c % 2
                for lt in range(m + 1):
                    for rk in range(4):
                        kt = 4 * lt + rk
                        P.dma("sp", kTb[bf][:, kt * 512:(kt + 1) * 512], kT_all[lt][rk * D + c * 128:rk * D + (c + 1) * 128, :],
                              [RkTall[lt]], [RkTb[bf]])
                        P.dma("sp", vvb[bf][:, 4 * kt:4 * kt + 4, :],
                              v_all[lt][rk * MT:(rk + 1) * MT, c * 128:(c + 1) * 128].rearrange("(kb p) d -> p kb d", p=128),
                              [Rvall[lt]], [Rvvb[bf]])

            load_kv(0)
            for c in range(NCH):
                if c + 1 < NCH:
                    load_kv(c + 1)
                bf = c % 2; kT = kTb[bf]; vv = vvb[bf]
                nblk = 4 * nkt
                blocks = []
                for bi in range(nblk):
                    B = nblk - 1 - bi
                    kt, kb = divmod(B, 4)
                    mask_ap = None
                    if kt >= 4 * m:
                        mi = (kt - 4 * m) * 4 + kb
                        mask_ap = masks[:, mi * 512:(mi + 1) * 512]
                    blocks.append({"K": 128, "kT": [kT[0:64, B * 128:(B + 1) * 128], kT[64:128, B * 128:(B + 1) * 128]],
                                   "v": vv[:, B, :], "mask": mask_ap, "restart52": False, "Rk": RkTb[bf], "Rv": Rvvb[bf]})
                emit_chains(blocks, [qT[0:64, c, 0:MT], qT[64:128, c, 0:MT]], MT)
                for hh in range(2):
                    ph = slice(64 * hh, 64 * hh + 64)
                    P.copy("dve", oT[ph, c, 0:MT], ps[6 + hh][ph, 0:MT], [Rps[6 + hh]], [RoT])

            if m == 0:
                off_k = NCH * NMAX * 2
                kTbig = AR[:, off_k:off_k + 16384].rearrange("p (c k) -> p c k", c=NCH)
                vbig = AR[:, off_k + 16384:off_k + 32768].rearrange("p (b d) -> p b d", d=D)
                Rkbig = [RkTb[0], RkTb[1]]; Rvbig = [Rvvb[0], Rvvb[1]]
                NQ = 256
                for s_ in range(2):
                    q0 = MT + A_COLS[s_]
                    for kb_ in range(16):
                        for hf_ in range(2):
                            P.dma("pool", vbig[:, kb_, hf_ * 512:(hf_ + 1) * 512],
                                  cv_d[s_, kb_ * 128:(kb_ + 1) * 128, hf_ * 512:(hf_ + 1) * 512], [], [Rvbig])
                    for kb in range(16):
                        h2 = kb % 2
                        P.dma("sp", cst[:, h2 * 1024:(h2 + 1) * 1024], ck_d[s_, kb * 128:(kb + 1) * 128, :], [], [Rcst2[h2]])
                        banks = (5, 7)
                        for half in range(2):
                            bank = banks[half]
                            for cc in range(4):
                                c = half * 4 + cc
                                P.tr(ps[bank][:, cc * 128:(cc + 1) * 128], cst[:, h2 * 1024 + c * 128:h2 * 1024 + (c + 1) * 128],
                                     ident[:, :], [Rcst2[h2], Rid], [Rps[bank]])
                            P.copy("dve", kTbig[:, half * 4:half * 4 + 4, kb * 128:(kb + 1) * 128],
                                   ps[bank][:, :].rearrange("p (c k) -> p c k", c=4), [Rps[bank]], [Rkbig])
                    nb = 17
                    e_ = eb[0]; Re_ = Reb[0]; sp_ = spb[0]; Rsp_ = Rspb[0]; ex_ = exb[0]; Rex_ = Rexb[0]; ww_ = wwb[0]; Rww_ = Rwwb[0]

                    def blkK(a):
                        return SM if a == 0 else 128

                    def zmm_s(a):
                        K = blkK(a)
                        for hh in range(2):
                            Z = 2 * hh + a % 2
                            ph = slice(64 * hh, 64 * hh + 64)
                            for c in range(NCH):
                                if a == 0:
                                    kap = kTs[ph, c, 0:SM]; Rk = RkTs
                                else:
                                    B = 16 - a
                                    kap = kTbig[ph, c, B * 128:(B + 1) * 128]; Rk = Rkbig
                                P.mm(ps[Z][0:K, 16 * c:16 * c + 16], kap, qT[ph, c, q0:q0 + 16], True, True, [Rk, RqT], [Rps[Z]])

                    zmm_s(0)
                    for t in range(nb + 1):
                        b = t - 1
                        if b >= 0:
                            Kb = blkK(b); pb = b % 2
                            P.mm(ps[4][0:Kb, 0:NQ], uneg[0:Kb, 0:Kb], sp_[pb][0:Kb, 0:NQ], b == 0, True,
                                 [Rconst, Rsp_[pb]], [Rps[4]], skip_group_check=True)
                        if t + 1 < nb:
                            zmm_s(t + 1)
                        if t < nb:
                            K = blkK(t); pr = t % 2
                            for hh in range(2):
                                P.act(e_[pr][0:K, 128 * hh:128 * hh + 128], ps[2 * hh + pr][0:K, 0:128], AF.Exp,
                                      [Rps[2 * hh + pr]], [Re_[pr]])
                            if t == 0:
                                P.tt("pool", e_[pr][0:K, 0:NQ], e_[pr][0:K, 0:NQ], smask[0:SM, NQ * s_:NQ * (s_ + 1)], ALU.mult,
                                     [Re_[pr], Rmask], [Re_[pr]])
                        if b >= 0:
                            P.act(ex_[0:Kb, 0:NQ], ps[4][0:Kb, 0:NQ], AF.Exp, [Rps[4]], [Rex_])
                        if t < nb:
                            P.act(sp_[pr][0:K, 0:NQ], e_[pr][0:K, 0:NQ], AF.Ln, [Re_[pr]], [Rsp_[pr]], bias=1.0)
                        if b >= 0:
                            if b < nb - 1:
                                if b == 0:
                                    P.mm(ps[4][:, 0:NQ], nones[0:Kb, :], sp_[pb][0:Kb, 0:NQ], True, True,
                                         [Rconst, Rsp_[pb]], [Rps[4]], skip_group_check=True)
                                else:
                                    P.mm(ps[4][:, 0:NQ], ubar[:, :], sp_[pb][:, 0:NQ], False, True,
                                         [Rconst, Rsp_[pb]], [Rps[4]], skip_group_check=True)
                            P.tt("dve", ww_[0:Kb, 0:NQ].rearrange("p (c h q) -> p c h q", c=NCH, h=2),
                                 e_[pb][0:Kb, 0:NQ].rearrange("p (h c q) -> p c h q", h=2, c=NCH),
                                 ex_[0:Kb, 0:NQ].rearrange("p (h c q) -> p c h q", h=2, c=NCH), ALU.mult, [Re_[pb], Rex_], [Rww_])
                            for c in range(NCH):
                                if b == 0:
                                    vap = vs[0:SM, c * 128:(c + 1) * 128]; Rv = Rvs
                                else:
                                    B = 16 - b
                                    vap = vbig[:, B, c * 128:(c + 1) * 128]; Rv = Rvbig
                                P.mm(ps[6][:, 32 * c:32 * c + 32], vap, ww_[0:Kb, 32 * c:32 * c + 32], b == 0 and c == 0, b == nb - 1,
                                     [Rv, Rww_], [Rps[6]], skip_group_check=True)
                    Oview = ps[6][:, 0:NQ].rearrange("p (c h q) -> p c h q", c=NCH, h=2)
                    P.copy("dve", oT[0:64, :, q0:q0 + 16], Oview[0:64, :, 0, :], [Rps[6]], [RoT])
                    P.copy("dve", oT[64:128, :, q0:q0 + 16], Oview[64:128, :, 1, :], [Rps[6]], [RoT])

            for s in range(4):
                b = s % 2
                for dc in range(2):
                    dm = 2 * s + dc
                    for (n0, n) in chunks:
                        yb = 6 + k % 2; k += 1
                        for c in range(NCH):
                            P.mm(ps[yb][:, 0:n], wq[b][:, c, dc * 128:(dc + 1) * 128], oT[:, c, n0:n0 + n], c == 0, c == NCH - 1,
                                 [Rwq[b], RoT], [Rps[yb]])
                        P.tt("dve", r[:, dm, n0:n0 + n], ps[yb][:, 0:n], xa[:, dm, n0:n0 + n], ALU.add, [Rps[yb], Rxa[dm]], [Rr[dm]])
                        ln_prep(dm, n0, n)
                if s + 2 < 4:
                    P.dma("pool", wq[b], WO[:, :, (s + 2) * 256:(s + 3) * 256], [], [Rwq[b]])

        xa_d3 = xa_d.rearrange("p (c n) -> p c n", c=NCH)
        xb_d3 = xb_d.rearrange("p (c n) -> p c n", c=NCH)
        Rxad = [Res(f"xad{m}") for m in range(4)]
        Rxbd = [Res(f"xbd{m}") for m in range(4)]

        def pass_chunks(m):
            if m == 0:
                return [(0, MT), (MT, SM)], MT + SM
            return [(0, MT)], MT

        steps = []

        def PH(f):
            steps.append(("ph", f))

        def OT(f):
            steps.append(("ot", f))

        def load_x(m):
            for tb in range(4):
                load_tok(xin[MT * m + tb * 128:MT * m + (tb + 1) * 128, :], 128,
                         [(xa, Rxa, "act", ALPHA), (xb, Rxb, "dve", None)], tb * 128, None)
            if m == 0:
                load_tok(xin[2048:2048 + SM, :], SM, [(xa, Rxa, "act", ALPHA), (xb, Rxb, "dve", None)], MT, None)

        def bounce_out(m):
            P.dma("sp", xa_d3[:, :, MT * m:MT * (m + 1)], xa[:, :, 0:MT], [Rxa], [Rxad[m]])
            P.dma("sp", xb_d3[:, :, MT * m:MT * (m + 1)], xb[:, :, 0:MT], [Rxb], [Rxbd[m]])
            if m == 0:
                P.dma("sp", xa_d3[:, :, 2048:2048 + SM], xa[:, :, MT:MT + SM], [Rxa], [Rxad[m]])
                P.dma("sp", xb_d3[:, :, 2048:2048 + SM], xb[:, :, MT:MT + SM], [Rxb], [Rxbd[m]])

        def bounce_in(m):
            P.dma("sp", xa[:, :, 0:MT], xa_d3[:, :, MT * m:MT * (m + 1)], [Rxad[m]], [Rxa])
            P.dma("sp", xb[:, :, 0:MT], xb_d3[:, :, MT * m:MT * (m + 1)], [Rxbd[m]], [Rxb])
            if m == 0:
                P.dma("sp", xa[:, :, MT:MT + SM], xa_d3[:, :, 2048:2048 + SM], [Rxad[m]], [Rxa])
                P.dma("sp", xb[:, :, MT:MT + SM], xb_d3[:, :, 2048:2048 + SM], [Rxbd[m]], [Rxb])

        def store_y(m):
            for tb in range(4):
                store_tok(xa, tb * 128, 128, y_out[MT * m + tb * 128:MT * m + (tb + 1) * 128, :], Rxa)
            if m == 0:
                store_tok(xa, MT, SM, y_out[2048:2048 + SM, :], Rxa)

        def collectives(ms):
            groups = [[0, 1, 2, 3], [4, 5, 6, 7]]
            for m_ in ms:
                P.op("pool", lambda e, m_=m_: e.collective_compute("AllGather", ALU.bypass, replica_groups=groups,
                                                                   ins=[kT_in[m_][:, :]], outs=[kT_all[m_][:, :]]),
                     [RkTin[m_]], [RkTall[m_]], dma=True, cc=True)
                P.op("pool", lambda e, m_=m_: e.collective_compute("AllGather", ALU.bypass, replica_groups=groups,
                                                                   ins=[v_in[m_][:, :]], outs=[v_all[m_][:, :]]),
                     [Rvin[m_]], [Rvall[m_]], dma=True, cc=True)

        for m in range(n_pass):
            chunks, N = pass_chunks(m)
            OT(lambda m=m: load_x(m))
            for l in range(2):
                PH(lambda l=l, ch=chunks, N=N: ffn(l, 0, ch, N))
                OT(lambda l=l, ch=chunks, N=N: layernorm(3 * l + 0, ch, N))
                PH(lambda l=l, m=m, ch=chunks, N=N: conv(l, m, ch, N))
                OT(lambda l=l, ch=chunks, N=N: layernorm(3 * l + 1, ch, N))
                PH(lambda l=l, ch=chunks, N=N: ffn(l, 1, ch, N))
                OT(lambda l=l, ch=chunks, N=N: layernorm(3 * l + 2, ch, N))
            OT(lambda m=m: bounce_out(m))
            PH(lambda m=m, ch=chunks, N=N: kvproj(m, ch, N))
            if do_phase2:
                OT(lambda m=m: collectives([m]))
        if do_phase2:
            for m in range(n_pass):
                chunks, N = pass_chunks(m)
                OT(lambda m=m: bounce_in(m))
                for l in range(2, 4):
                    PH(lambda l=l, ch=chunks, N=N: ffn(l, 0, ch, N))
                    OT(lambda l=l, ch=chunks, N=N: layernorm(3 * l + 0, ch, N))
                    PH(lambda l=l, m=m, ch=chunks, N=N: attn(l, m, ch, N))
                    OT(lambda l=l, ch=chunks, N=N: layernorm(3 * l + 1, ch, N))
                    PH(lambda l=l, ch=chunks, N=N: ffn(l, 1, ch, N))
                    OT(lambda l=l, ch=chunks, N=N: layernorm(3 * l + 2, ch, N, last=(l == 3)))
                OT(lambda m=m: store_y(m))

        gens = {}
        ph_idx = [i_ for i_, st_ in enumerate(steps) if st_[0] == "ph"]

        def begin(i_):
            g_ = steps[i_][1]()
            next(g_)
            gens[i_] = g_

        if ph_idx:
            begin(ph_idx[0])
        for i_, (kind_, f_) in enumerate(steps):
            if kind_ == "ot":
                f_()
            else:
                for _ in gens.pop(i_):
                    pass
                nxt_ = [j_ for j_ in ph_idx if j_ > i_]
                if nxt_:
                    begin(nxt_[0])
        P.emit()
    return nc


_NC_CACHE = {}


def _host_inputs(inp):
    f = np.float32
    xp = np.asarray(inp["x_prompt"], f); xs = np.asarray(inp["x_sample"], f)
    ck = np.asarray(inp["cache_k"], f).reshape(16, 2048, D); cv = np.asarray(inp["cache_v"], f).reshape(16, 2048, D)
    sc = np.asarray(inp["state_conv"], f)
    lng = np.asarray(inp["ln_g"], f).reshape(12, D); lnb = np.asarray(inp["ln_b"], f).reshape(12, D)
    wc = np.asarray(inp["w_conv"], f).reshape(6, D)
    shared = {k: np.ascontiguousarray(np.asarray(inp[k], f)) for k in
              ("w_ffn_gate", "w_ffn_up", "w_ffn_down", "w_conv_in", "w_conv_out", "w_kv", "w_q", "w_o")}
    smask = np.zeros((SM, 512), f)
    for s_ in range(2):
        for h in range(16):
            for t in range(16):
                for kk in range(t):
                    smask[A_COLS[s_] + kk, 256 * s_ + 16 * h + t] = 1.0
    maps = []
    sidx = np.arange(128)[:, None]; tidx = np.arange(512)[None, :]
    for c in range(8):
        b, j = divmod(c, 4)
        xin = np.zeros((NTOK, D), f)
        for m in range(4):
            t0 = 512 * (4 * m + j)
            xin[512 * m:512 * (m + 1)] = xp[b, t0:t0 + 512]
            if t0 > 0:
                xin[2048 + 4 * m:2048 + 4 * m + 4] = xp[b, t0 - 4:t0]
        for s_ in range(2):
            xin[2048 + A_COLS[s_]:2048 + A_COLS[s_] + 16] = xs[2 * c + s_]
        prm = np.concatenate([lng, lnb, wc, sc[:, 2 * c:2 * c + 2].reshape(8, D)], 0)
        masks = np.zeros((128, 16 * 512), f)
        for i in range(4):
            for kb in range(4):
                mi = i * 4 + kb
                masks[:, mi * 512:(mi + 1) * 512] = ((512 * i + 128 * kb + sidx) < (512 * j + tidx)).astype(f)
        hm = np.full((128, 1), 0.0 if j == 0 else 1.0, f)
        d = {"xin": xin, "prm": np.ascontiguousarray(prm), "masks": masks, "smask": smask, "hmask": hm,
             "cache_k": np.ascontiguousarray(ck[2 * c:2 * c + 2]), "cache_v": np.ascontiguousarray(cv[2 * c:2 * c + 2])}
        d.update(shared)
        maps.append(d)
    return maps


def _assemble(results):
    f = np.float32
    y_p = np.zeros((2, 8192, D), f); k_p = np.zeros((2, 8192, D), f); v_p = np.zeros((2, 8192, D), f)
    y_s = np.zeros((16, 16, D), f); k_s = np.zeros((16, 16, D), f); v_s = np.zeros((16, 16, D), f)
    conv_p = np.zeros((2, 2, 2, D), f); conv_s = np.zeros((2, 16, 2, D), f)
    for c in range(8):
        b, j = divmod(c, 4)
        rr = results[c]
        for m in range(4):
            t0 = 512 * (4 * m + j)
            y_p[b, t0:t0 + 512] = rr["y_out"][512 * m:512 * (m + 1)]
            k_p[b, t0:t0 + 512] = rr["k_out"][512 * m:512 * (m + 1)]
            v_p[b, t0:t0 + 512] = rr["v_out"][512 * m:512 * (m + 1)]
        for s_ in range(2):
            a0 = 2048 + A_COLS[s_]
            y_s[2 * c + s_] = rr["y_out"][a0:a0 + 16]
            k_s[2 * c + s_] = rr["k_out"][a0:a0 + 16]
            v_s[2 * c + s_] = rr["v_out"][a0:a0 + 16]
            conv_s[:, 2 * c + s_] = rr["conv_out"][:, 2 + 2 * s_:4 + 2 * s_]
        if j == 3:
            conv_p[:, b] = rr["conv_out"][:, 0:2]
    return (y_p, y_s, k_p.reshape(2, 8192, 16, 64), v_p.reshape(2, 8192, 16, 64), conv_p,
            k_s.reshape(16, 16, 16, 64), v_s.reshape(16, 16, 16, 64), conv_s)


def kernel(**inputs):
    if "nc" not in _NC_CACHE:
        _NC_CACHE["nc"] = build()
    nc = _NC_CACHE["nc"]
    maps = _host_inputs(inputs)
    res = run_bass_kernel_spmd(nc, maps, core_ids=list(range(8)))
    return _assemble(res.results)
```

```python
import functools
import numpy as np
import concourse.bass as bass
import concourse.mybir as mybir

F32 = mybir.dt.float32
BF16 = mybir.dt.bfloat16
I32 = mybir.dt.int32
AF = mybir.ActivationFunctionType
ALU = mybir.AluOpType

ENGS = ("pe", "act", "dve", "pool", "sp")
EPOCH = 30000


class Res:
    __slots__ = ("name", "w", "rd", "rd_dma")

    def __init__(self, name=""):
        self.name = name
        self.w = None
        self.rd = {}
        self.rd_dma = []


class Op:
    __slots__ = ("eng", "fn", "deps", "dma", "sig", "sigidx", "dsem", "dtarget", "dprev", "pos", "inc")

    def __init__(self, eng, fn, dma, inc=16):
        self.inc = inc
        self.eng = eng
        self.fn = fn
        self.dma = dma
        self.deps = []
        self.sig = False
        self.sigidx = None
        self.dsem = None
        self.dtarget = None
        self.dprev = None
        self.pos = None


class Prog:
    def __init__(self, nc, n_dsem=None):
        self.nc = nc
        self.ops = {e: [] for e in ENGS}
        self.n_dsem = n_dsem or {"sp": 24, "pool": 12, "act": 4, "cc": 2}
        self.dma_rr = {e: 0 for e in self.n_dsem}
        self.dma_last = {e: [None] * n for e, n in self.n_dsem.items()}
        self.dma_tgt = {e: [0] * n for e, n in self.n_dsem.items()}
        self.all_dma = []

    def op(self, eng, fn, reads=(), writes=(), dma=False, cc=False):
        o = Op(eng, fn, dma, 1 if cc else 16)

        def _flat(xs):
            out = []
            for x_ in xs:
                if isinstance(x_, (list, tuple)):
                    out.extend(_flat(x_))
                elif x_ is not None:
                    out.append(x_)
            return out
        reads = _flat(reads); writes = _flat(writes)
        deps = []
        for r in reads:
            if r.w is not None:
                deps.append(r.w)
        for w in writes:
            if w.w is not None:
                deps.append(w.w)
            deps.extend(w.rd.values())
            deps.extend(w.rd_dma)
        for r in reads:
            if dma:
                r.rd_dma.append(o)
            else:
                r.rd[eng] = o
        for w in writes:
            w.w = o
            w.rd = {}
            w.rd_dma = []
        seen = set()
        for d in deps:
            if d is o or id(d) in seen:
                continue
            seen.add(id(d))
            if (not d.dma) and (not dma) and d.eng == "pe" and eng == "pe":
                continue
            o.deps.append(d)
            if not d.dma:
                d.sig = True
        if dma:
            pl = "cc" if cc else eng
            k = self.dma_rr[pl]
            self.dma_rr[pl] = (k + 1) % self.n_dsem[pl]
            o.dsem = (pl, k)
            o.dprev = self.dma_tgt[pl][k]
            self.dma_tgt[pl][k] += o.inc
            o.dtarget = self.dma_tgt[pl][k]
            self.all_dma.append(o)
        o.pos = len(self.ops[eng])
        self.ops[eng].append(o)
        return o

    def mm(self, out, lhsT, rhs, start, stop, reads, writes, **kw):
        return self.op("pe", lambda e: e.matmul(out, lhsT, rhs, start=start, stop=stop, **kw), reads, writes)

    def tr(self, out, in_, ident, reads, writes):
        return self.op("pe", lambda e: e.transpose(out, in_, ident), reads, writes)

    def act(self, out, in_, func, reads, writes, eng="act", **kw):
        fname = getattr(func, "name", str(func))
        if fname in ("Copy", "Identity"):
            sc = kw.get("scale"); bi = kw.get("bias")
            if sc is None and bi is None:
                return self.op("dve", lambda e: e.tensor_copy(out, in_), reads, writes)
            if bi is None:
                return self.op("dve", lambda e: e.tensor_scalar(out, in_, sc, None, ALU.mult), reads, writes)
            sc2 = 1.0 if sc is None else sc
            return self.op("dve", lambda e: e.tensor_scalar(out, in_, sc2, bi, ALU.mult, ALU.add), reads, writes)
        if fname == "Square":
            return self.op("dve", lambda e: e.tensor_tensor(out, in_, in_, ALU.mult), reads, writes)
        return self.op(eng, lambda e: e.activation(out, in_, func, **kw), reads, writes)

    def tt(self, eng, out, in0, in1, op, reads, writes):
        return self.op(eng, lambda e: e.tensor_tensor(out, in0, in1, op), reads, writes)

    def ts(self, eng, out, in0, s1, s2, op0, op1, reads, writes):
        if op1 is None:
            return self.op(eng, lambda e: e.tensor_scalar(out, in0, s1, None, op0), reads, writes)
        return self.op(eng, lambda e: e.tensor_scalar(out, in0, s1, s2, op0, op1), reads, writes)

    def stt(self, eng, out, in0, scalar, in1, op0, op1, reads, writes):
        return self.op(eng, lambda e: e.scalar_tensor_tensor(out, in0, scalar, in1, op0, op1), reads, writes)

    def copy(self, eng, out, in_, reads, writes):
        if eng == "act":
            eng = "dve"
        return self.op(eng, lambda e: e.tensor_copy(out, in_), reads, writes)

    def memset(self, eng, ap, val, writes):
        return self.op(eng, lambda e: e.memset(ap, val), (), writes)

    def dma(self, q, out, in_, reads, writes, **kw):
        return self.op(q, lambda e: e.dma_start(out=out, in_=in_, **kw), reads, writes, dma=True)

    def emit(self):
        nc = self.nc
        import contextlib
        with contextlib.ExitStack() as st:
            csems = {}
            for e in ("pe", "act", "dve", "pool"):
                n = 0
                for o in self.ops[e]:
                    if o.sig:
                        o.sigidx = n
                        n += 1
                nep = max(1, (n + EPOCH - 1) // EPOCH)
                csems[e] = [st.enter_context(nc.semaphore(f"c_{e}_{i}")) for i in range(nep)]
            dsems = {}
            for q, n in self.n_dsem.items():
                dsems[q] = [st.enter_context(nc.semaphore(f"d_{q}_{i}")) for i in range(n)]
            block = st.enter_context(nc.Block())

            def run_engine(ename, eng):
                waited_c = {}
                waited_d = {}

                def wait_dep(d):
                    if d.dma:
                        key = d.dsem
                        if waited_d.get(key, 0) >= d.dtarget:
                            return
                        waited_d[key] = d.dtarget
                        eng.wait_ge(dsems[key[0]][key[1]], d.dtarget)
                    else:
                        if waited_c.get(d.eng, -1) >= d.sigidx:
                            return
                        waited_c[d.eng] = d.sigidx
                        ep, v = divmod(d.sigidx, EPOCH)
                        eng.wait_ge(csems[d.eng][ep], v + 1)

                for o in self.ops[ename]:
                    for d in o.deps:
                        wait_dep(d)
                    if o.dma:
                        q, k = o.dsem
                        if o.dprev > 0 and waited_d.get(o.dsem, 0) < o.dprev:
                            waited_d[o.dsem] = o.dprev
                            eng.wait_ge(dsems[q][k], o.dprev)
                        ins = o.fn(eng)
                        ins.then_inc(dsems[q][k], o.inc)
                    else:
                        ins = o.fn(eng)
                        if o.sig:
                            ep, v = divmod(o.sigidx, EPOCH)
                            ins.then_inc(csems[ename][ep], 1)
                if ename == "sp":
                    for q, n in self.n_dsem.items():
                        for k in range(n):
                            t = self.dma_tgt[q][k]
                            if t > 0 and waited_d.get((q, k), 0) < t:
                                eng.wait_ge(dsems[q][k], t)

            @block.tensor
            def _(e):
                run_engine("pe", e)

            @block.scalar
            def _(e):
                run_engine("act", e)

            @block.vector
            def _(e):
                run_engine("dve", e)

            @block.gpsimd
            def _(e):
                run_engine("pool", e)

            @block.sync
            def _(e):
                run_engine("sp", e)


import contextlib
from concourse.bass_utils import run_bass_kernel_spmd

D = 1024
NCH = 8
FF = 2816
NF = 22
MT = 512
SM = 52
NTOK = 2048 + SM
ALPHA = 8.0 ** 0.25
EPS = 1e-5
A_COLS = (18, 36)


def build(n_pass=4, do_phase2=True, stage=99):
    nc = bass.Bass("TRN2", target_bir_lowering=False)

    import os as _os
    TINY = _os.environ.get("KTINY", "") == "1"

    def din(name, shape, dt=F32):
        if TINY and (name.startswith("w_") or name.startswith("cache")):
            shape = [2] * len(shape)
        return nc.dram_tensor(name, shape, dt, kind="ExternalInput").ap()

    def dout(name, shape, dt=F32):
        return nc.dram_tensor(name, shape, dt, kind="ExternalOutput").ap()

    def dint(name, shape, dt):
        return nc.dram_tensor(name, shape, dt).ap()

    xin = din("xin", [NTOK, D])
    prm = din("prm", [38, D])
    masks_d = din("masks", [128, 16 * 512])
    smask_d = din("smask", [SM, 512])
    hmask_d = din("hmask", [128, 1])
    ck_d = din("cache_k", [2, 2048, D])
    cv_d = din("cache_v", [2, 2048, D])
    w_gate = din("w_ffn_gate", [4, 2, D, FF])
    w_up = din("w_ffn_up", [4, 2, D, FF])
    w_down = din("w_ffn_down", [4, 2, FF, D])
    w_cin = din("w_conv_in", [2, D, 3 * D])
    w_cout = din("w_conv_out", [2, D, D])
    w_kv = din("w_kv", [D, 2 * D])
    w_q = din("w_q", [2, D, D])
    w_o = din("w_o", [2, D, D])
    y_out = dout("y_out", [NTOK, D])
    k_out = dout("k_out", [NTOK, D])
    v_out = dout("v_out", [NTOK, D])
    conv_out = dout("conv_out", [2, 6, D])
    kT_in = [dint(f"kT_in{m_}", [D, MT], BF16) for m_ in range(4)]
    v_in = [dint(f"v_in{m_}", [MT, D], BF16) for m_ in range(4)]
    kT_all = [dint(f"kT_all{m_}", [4 * D, MT], BF16) for m_ in range(4)]
    v_all = [dint(f"v_all{m_}", [4 * MT, D], BF16) for m_ in range(4)]
    xa_d = dint("xa_d", [128, NCH * NTOK], F32)
    xb_d = dint("xb_d", [128, NCH * NTOK], BF16)

    with contextlib.ExitStack() as st:
        def sb(name, shape, dt):
            return st.enter_context(nc.sbuf_tensor(name, shape, dt))

        NMAX = 576
        P = Prog(nc)
        ps = [st.enter_context(nc.psum_tensor(f"ps{i}", [128, 512], F32)) for i in range(8)]
        Rps = [Res(f"ps{i}") for i in range(8)]

        def v3(t, n):
            return t[:, 0:NCH * n].rearrange("p (c n) -> p c n", c=NCH)

        xa_t = sb("xa", [128, NCH * NMAX], F32); xa = v3(xa_t, NMAX); Rxa = [Res(f"xa{c_}") for c_ in range(NCH)]
        xb_t = sb("xb", [128, NCH * NMAX], BF16); xb = v3(xb_t, NMAX); Rxb = [Res(f"xb{c_}") for c_ in range(NCH)]
        r = xa; Rr = Rxa; rb = xb; Rrb = Rxb
        rq_t = sb("rq", [128, NCH * NMAX], BF16); rq = v3(rq_t, NMAX); Rrq = Res("rq")
        mean_s = sb("mean_s", [128, NMAX], F32); Rmean = Res("mean")
        rstd_s = sb("rstd_s", [128, NMAX], F32); Rrstd = Res("rstd")
        m2_s = sb("m2_s", [128, NMAX], F32); Rm2 = Res("m2")
        ident = sb("ident", [128, 128], F32); Rid = Res("ident")
        uneg = sb("uneg", [128, 128], BF16); ubar = sb("ubar", [128, 128], BF16)
        nones = sb("nones", [128, 128], BF16); meanm = sb("meanm", [128, 128], BF16)
        Rconst = Res("const")
        par_t = sb("par", [128, NCH * 38], F32); par = par_t[:, :].rearrange("p (c n) -> p c n", c=NCH); Rpar = Res("par")
        apar_t = sb("apar", [128, NCH * 24], F32); apar = apar_t[:, :].rearrange("p (c n) -> p c n", c=NCH)
        hmask = sb("hmask_sb", [128, 1], F32)
        masks = sb("masks_sb", [128, 16 * 512], BF16); Rmask = Res("masks")
        smask = sb("smask_sb", [SM, 512], BF16)
        stg = [sb(f"stg{i}", [128, D], F32) for i in range(2)]; Rstg = [Res("stg0"), Res("stg1")]
        ubm2 = [sb(f"ubm{i}", [128, 514], F32) for i in range(2)]; Rubm2 = [Res("ubm0"), Res("ubm1")]
        ulast_t = sb("ulast", [128, NCH * 32], F32); ulast = v3(ulast_t, 32); Rulast = Res("ulast")
        ubs_t = sb("ubs", [128, NCH * 96], F32); ubs = v3(ubs_t, 96); Rubs = Res("ubs")
        carry_t = sb("carry", [128, 2 * NCH * 8], F32)
        carry = carry_t[:, :].rearrange("p (l c m t) -> p l c m t", l=2, c=NCH, m=4); Rcarry = Res("carry")
        tmpf = [sb(f"tmpf{i}", [128, 512], F32) for i in range(2)]; Rtmpf = [Res("tf0"), Res("tf1")]
        ctmp = [sb(f"ctmp{i}", [128, 512], F32) for i in range(2)]; Rctmp = [Res("ct0"), Res("ct1")]
        kTs_t = sb("kTs", [128, NCH * SM], BF16); kTs = v3(kTs_t, SM); RkTs = Res("kTs")
        vs = sb("vs", [SM, D], BF16); Rvs = Res("vs")
        cst = sb("cst", [128, 16 * 128], F32); Rcst2 = [Res("cst0"), Res("cst1")]
        AR = sb("arena", [128, 52224], BF16)

        arena_res = []

        def arena_switch(layout):
            old_ops = []
            for rr in arena_res:
                if rr.w is not None:
                    old_ops.append(rr.w)
                old_ops.extend(rr.rd.values())
                old_ops.extend(rr.rd_dma)
            latest = {}
            dmas = []
            seen = set()
            for o in old_ops:
                if id(o) in seen:
                    continue
                seen.add(id(o))
                if o.dma:
                    dmas.append(o)
                else:
                    if o.eng not in latest or latest[o.eng].pos < o.pos:
                        latest[o.eng] = o
            fence = list(latest.values()) + dmas
            del arena_res[:]
            aps, res = {}, {}
            off = 0
            for name, n in layout:
                aps[name] = AR[:, off:off + n]
                rr = Res(name)
                rr.rd_dma = list(fence)
                res[name] = rr
                arena_res.append(rr)
                off += n
            assert off <= 52224, off
            return aps, res

        psc = {"i": 0}

        P.memset("pool", stg[0][:], 0.0, [Rstg[0]])
        P.memset("pool", stg[1][:], 0.0, [Rstg[1]])
        P.memset("pool", xa_t[:], 0.0, [Rxa])
        P.memset("pool", xb_t[:], 0.0, [Rxb])
        P.memset("pool", ubs_t[:], 0.0, [Rubs])
        P.memset("pool", ulast_t[:], 0.0, [Rulast])
        P.memset("pool", ident[:], 0.0, [Rid])
        P.op("pool", lambda e: e.affine_select(ident[:], ident[:], [[-1, 128]], ALU.not_equal, 1.0, base=0, channel_multiplier=1), [Rid], [Rid])
        import os
        SK = os.environ.get("KSKIP", "").split(",")
        if "consts" not in SK:
            P.memset("pool", uneg[:], -1.0, [Rconst])
            P.op("pool", lambda e: e.affine_select(uneg[:], uneg[:], [[-1, 128]], ALU.is_ge, 0.0, base=0, channel_multiplier=1), [Rconst], [Rconst])
            P.memset("pool", nones[:], -1.0, [Rconst])
            P.tt("pool", ubar[:], nones[:], uneg[:], ALU.subtract, [Rconst], [Rconst])
            P.memset("pool", meanm[:], 1.0 / D, [Rconst])
        if "masks" not in SK:
            for mi_ in range(16):
                P.dma("pool", masks[:, mi_ * 512:(mi_ + 1) * 512], masks_d[:, mi_ * 512:(mi_ + 1) * 512], [], [Rmask])
        if "smask" not in SK:
            P.dma("pool", smask[:], smask_d[:, :], [], [Rmask])
        if "hmask" not in SK:
            P.dma("sp", hmask[:], hmask_d[:, :], [], [Rpar])

        def load_tok(rows_ap, T, dst3, t0, Rdst, scale_dst=None):
            k = psc["i"] % 2; psc["i"] += 1
            Tp = ((T + 31) // 32) * 32
            P.dma("sp", stg[k][0:T, :], rows_ap, [], [Rstg[k]])
            for half in range(2):
                bank = 6 + half
                for cc in range(4):
                    c = half * 4 + cc
                    P.tr(ps[bank][:, cc * 128:cc * 128 + Tp], stg[k][0:Tp, c * 128:(c + 1) * 128], ident[0:Tp, 0:Tp],
                         [Rstg[k], Rid], [Rps[bank]])
                src = ps[bank][:, :].rearrange("p (c n) -> p c n", c=4)[:, :, 0:T]
                for (d3, Rd, eng, sc) in dst3:
                    dd = d3[:, half * 4:half * 4 + 4, t0:t0 + T]
                    if eng == "act":
                        if sc is None:
                            P.act(dd, src, AF.Copy, [Rps[bank]], [Rd])
                        else:
                            P.act(dd, src, AF.Identity, [Rps[bank]], [Rd], scale=sc)
                    else:
                        if sc is None:
                            P.copy(eng, dd, src, [Rps[bank]], [Rd])
                        else:
                            P.ts(eng, dd, src, sc, None, ALU.mult, None, [Rps[bank]], [Rd])

        def store_tok(src3, t0, T, rows_ap, Rsrc):
            k = psc["i"] % 2; psc["i"] += 1
            Tp = ((T + 31) // 32) * 32
            for half in range(2):
                bank = 6 + half
                for cc in range(4):
                    c = half * 4 + cc
                    P.tr(ps[bank][0:Tp, cc * 128:(cc + 1) * 128], src3[:, c, t0:t0 + Tp], ident[:, :],
                         [Rsrc, Rid], [Rps[bank]])
                if half == 0:
                    P.copy("dve", stg[k][0:T, 0:512], ps[bank][0:T, :], [Rps[bank]], [Rstg[k]])
                else:
                    P.act(stg[k][0:T, 512:1024], ps[bank][0:T, :], AF.Copy, [Rps[bank]], [Rstg[k]])
            P.dma("sp", rows_ap, stg[k][0:T, :], [Rstg[k]], [Res()])

        if "params" not in SK:
            load_tok(prm[:, :], 38, [(par, Rpar, "dve", None)], 0, Rpar)
            P.ts("dve", apar[:, :, :], par[:, :, 0:24], ALPHA, None, ALU.mult, None, [Rpar], [Rpar])

        def gcol(c, idx):
            return par[:, c, idx:idx + 1]

        def ln_prep(dm, n0, n):
            P.copy("dve", xb[:, dm, n0:n0 + n], xa[:, dm, n0:n0 + n], [Rxa[dm]], [Rxb[dm]])
            P.tt("dve", rq[:, dm, n0:n0 + n], xa[:, dm, n0:n0 + n], xa[:, dm, n0:n0 + n], ALU.mult, [Rxa[dm]], [Rrq])

        def layernorm(lidx, chunks, N, last=False):
            for (n0, n) in chunks:
                for c in range(NCH):
                    P.mm(ps[6][:, 0:n], meanm[:], rb[:, c, n0:n0 + n], c == 0, c == NCH - 1, [Rconst, Rrb], [Rps[6]])
                for c in range(NCH):
                    P.mm(ps[7][:, 0:n], meanm[:], rq[:, c, n0:n0 + n], c == 0, c == NCH - 1, [Rconst, Rrq], [Rps[7]])
                P.act(mean_s[:, n0:n0 + n], ps[6][:, 0:n], AF.Copy, [Rps[6]], [Rmean])
                P.tt("dve", m2_s[:, n0:n0 + n], mean_s[:, n0:n0 + n], mean_s[:, n0:n0 + n], ALU.mult, [Rmean], [Rm2])
                P.tt("dve", m2_s[:, n0:n0 + n], ps[7][:, 0:n], m2_s[:, n0:n0 + n], ALU.subtract, [Rps[7], Rm2], [Rm2])
                P.ts("dve", m2_s[:, n0:n0 + n], m2_s[:, n0:n0 + n], 0.0, EPS, ALU.max, ALU.add, [Rm2], [Rm2])
                P.act(rstd_s[:, n0:n0 + n], m2_s[:, n0:n0 + n], AF.Ln, [Rm2], [Rrstd])
                P.act(rstd_s[:, n0:n0 + n], rstd_s[:, n0:n0 + n], AF.Exp, [Rrstd], [Rrstd], scale=-0.5)
            for c0 in range(0, NCH, 2):
                cs_ = (c0, c0 + 1)
                for c in cs_:
                    P.tt("dve", r[:, c, 0:N], r[:, c, 0:N], mean_s[:, 0:N], ALU.subtract, [Rr[c], Rmean], [Rr[c]])
                for c in cs_:
                    P.tt("dve", r[:, c, 0:N], r[:, c, 0:N], rstd_s[:, 0:N], ALU.mult, [Rr[c], Rrstd], [Rr[c]])
                for c in cs_:
                    P.act(xb[:, c, 0:N], r[:, c, 0:N], AF.Identity, [Rr[c], Rpar], [Rxb[c]],
                          scale=par[:, c, lidx:lidx + 1], bias=par[:, c, 12 + lidx:13 + lidx])
                for c in cs_:
                    if last:
                        P.ts("pool", xa[:, c, 0:N], r[:, c, 0:N], par[:, c, lidx:lidx + 1], par[:, c, 12 + lidx:13 + lidx],
                             ALU.mult, ALU.add, [Rr[c], Rpar], [Rxa[c]])
                    else:
                        P.ts("pool", xa[:, c, 0:N], r[:, c, 0:N], apar[:, c, lidx:lidx + 1], apar[:, c, 12 + lidx:13 + lidx],
                             ALU.mult, ALU.add, [Rr[c], Rpar], [Rxa[c]])

        def ffn(l, i, chunks, N):
            NBG, NBD = 4, 3
            aps, res = arena_switch([(f"wg{i_}", 2048) for i_ in range(NBG)] + [(f"wu{i_}", 2048) for i_ in range(NBG)] +
                                    [(f"wd{i_}", 5632) for i_ in range(NBD)] + [("hT", NF * NMAX)])
            wg = [aps[f"wg{i_}"].rearrange("p (c f) -> p c f", c=NCH) for i_ in range(NBG)]
            wu = [aps[f"wu{i_}"].rearrange("p (c f) -> p c f", c=NCH) for i_ in range(NBG)]
            wd = [aps[f"wd{i_}"].rearrange("p (f d) -> p f d", f=NF) for i_ in range(NBD)]
            hT = aps["hT"].rearrange("p (f n) -> p f n", f=NF)
            Rwg = [res[f"wg{i_}"] for i_ in range(NBG)]; Rwu = [res[f"wu{i_}"] for i_ in range(NBG)]
            Rwd = [res[f"wd{i_}"] for i_ in range(NBD)]; RhT = res["hT"]
            WG = w_gate[l, i].rearrange("(c p) f -> p c f", p=128)
            WU = w_up[l, i].rearrange("(c p) f -> p c f", p=128)
            WD = w_down[l, i].rearrange("(f p) d -> p f d", p=128)

            def load_gu(s):
                b = s % NBG
                P.dma("pool", wg[b], WG[:, :, s * 256:(s + 1) * 256], [], [Rwg[b]])
                P.dma("pool", wu[b], WU[:, :, s * 256:(s + 1) * 256], [], [Rwu[b]])

            def load_d(s):
                b = s % NBD
                P.dma("pool", wd[b], WD[:, :, s * 256:(s + 1) * 256], [], [Rwd[b]])

            for s_ in range(NBG):
                load_gu(s_)
            yield
            k = 0
            for s in range(11):
                b = s % NBG
                for fc in range(2):
                    f = 2 * s + fc
                    for (n0, n) in chunks:
                        gbk = k % 2; ubk = 2 + k % 2; k += 1
                        for c in range(NCH):
                            P.mm(ps[gbk][:, 0:n], wg[b][:, c, fc * 128:(fc + 1) * 128], xb[:, c, n0:n0 + n],
                                 c == 0, c == NCH - 1, [Rwg[b], Rxb], [Rps[gbk]])
                        for c in range(NCH):
                            P.mm(ps[ubk][:, 0:n], wu[b][:, c, fc * 128:(fc + 1) * 128], xb[:, c, n0:n0 + n],
                                 c == 0, c == NCH - 1, [Rwu[b], Rxb], [Rps[ubk]])
                        tf = k % 2
                        P.act(tmpf[tf][:, 0:n], ps[gbk][:, 0:n], AF.Silu, [Rps[gbk]], [Rtmpf[tf]])
                        P.tt("dve", hT[:, f, n0:n0 + n], tmpf[tf][:, 0:n], ps[ubk][:, 0:n], ALU.mult,
                             [Rtmpf[tf], Rps[ubk]], [RhT])
                if s + NBG < 11:
                    load_gu(s + NBG)
                if s < NBD:
                    load_d(s)
            for s in range(4):
                b = s % NBD
                for dc in range(2):
                    dm = 2 * s + dc
                    for (n0, n) in chunks:
                        yb = 4 + k % 2; k += 1
                        for f in range(NF):
                            P.mm(ps[yb][:, 0:n], wd[b][:, f, dc * 128:(dc + 1) * 128], hT[:, f, n0:n0 + n],
                                 f == 0, f == NF - 1, [Rwd[b], RhT], [Rps[yb]])
                        P.stt("dve", r[:, dm, n0:n0 + n], ps[yb][:, 0:n], 0.5, xa[:, dm, n0:n0 + n], ALU.mult, ALU.add,
                              [Rps[yb], Rxa[dm]], [Rr[dm]])
                        ln_prep(dm, n0, n)
                if s + NBD < 4:
                    load_d(s + NBD)

        def conv(l, m, chunks, N):
            aps, res = arena_switch([("win0", 3072), ("win1", 3072), ("gT", NCH * NMAX), ("wo0", 2048), ("wo1", 2048)])
            win = [aps["win0"].rearrange("p (c g f) -> p c g f", c=NCH, g=3), aps["win1"].rearrange("p (c g f) -> p c g f", c=NCH, g=3)]
            gT = aps["gT"].rearrange("p (c n) -> p c n", c=NCH)
            wo = [aps["wo0"].rearrange("p (c f) -> p c f", c=NCH), aps["wo1"].rearrange("p (c f) -> p c f", c=NCH)]
            Rwin = [res["win0"], res["win1"]]; RgT = res["gT"]; Rwo = [res["wo0"], res["wo1"]]
            WI = w_cin[l].rearrange("(c p) (g i f) -> p c g i f", p=128, g=3, i=NCH)
            WO = w_cout[l].rearrange("(c p) d -> p c d", p=128)

            def load_in(i):
                b = i % 2
                for g in range(3):
                    P.dma("pool", win[b][:, :, g, :], WI[:, :, g, i, :], [], [Rwin[b]])

            def load_o(s):
                P.dma("pool", wo[s % 2], WO[:, :, s * 256:(s + 1) * 256], [], [Rwo[s % 2]])

            load_in(0)
            load_in(1)
            load_o(0)
            load_o(1)
            yield
            order = list(reversed(chunks))
            if m == 0:
                P.memset("pool", gT[:, :, MT:MT + 2], 0.0, [RgT])
            k = 0
            for i in range(NCH):
                b = i % 2
                w0 = par[:, i, 24 + 3 * l:25 + 3 * l]; w1 = par[:, i, 25 + 3 * l:26 + 3 * l]; w2 = par[:, i, 26 + 3 * l:27 + 3 * l]
                for (n0, n) in order:
                    small = n0 >= MT
                    base = 3 * (k % 2); k += 1
                    for g in range(3):
                        bank = base + g
                        for c in range(NCH):
                            P.mm(ps[bank][:, 0:n], win[b][:, c, g, :], xb[:, c, n0:n0 + n], c == 0, c == NCH - 1,
                                 [Rwin[b], Rxb], [Rps[bank]])
                    tf = k % 2
                    P.act(tmpf[tf][:, 0:n], ps[base + 1][:, 0:n], AF.Copy, [Rps[base + 1]], [Rtmpf[tf]])
                    ct = ctmp[tf]; Rct = Rctmp[tf]
                    if small:
                        P.tt("dve", ubs[:, i, 0:SM], tmpf[tf][:, 0:n], ps[base + 2][:, 0:n], ALU.mult,
                             [Rtmpf[tf], Rps[base + 2]], [Rubs])
                        for s_ in range(2):
                            pc = A_COLS[s_] - 2
                            P.copy("pool", ubs[:, i, pc:pc + 2], par[:, i, 30 + 4 * l + 2 * s_:32 + 4 * l + 2 * s_], [Rpar], [Rubs])
                        nn = SM - 2
                        P.ts("dve", ct[:, 0:nn], ubs[:, i, 0:nn], w0, None, ALU.mult, None, [Rubs, Rpar], [Rct])
                        P.stt("dve", ct[:, 0:nn], ubs[:, i, 1:nn + 1], w1, ct[:, 0:nn], ALU.mult, ALU.add, [Rubs, Rpar, Rct], [Rct])
                        P.stt("dve", ct[:, 0:nn], ubs[:, i, 2:nn + 2], w2, ct[:, 0:nn], ALU.mult, ALU.add, [Rubs, Rpar, Rct], [Rct])
                        P.tt("dve", gT[:, i, MT + 2:MT + SM], ct[:, 0:nn], ps[base][:, 2:SM], ALU.mult, [Rct, Rps[base]], [RgT])
                        for mm_ in range(4):
                            if mm_ == 0:
                                P.ts("pool", carry[:, l, i, 0, :], ubs[:, i, 2:4], hmask[:, 0:1], None, ALU.mult, None, [Rubs, Rpar], [Rcarry])
                            else:
                                P.copy("pool", carry[:, l, i, mm_, :], ubs[:, i, 4 * mm_ + 2:4 * mm_ + 4], [Rubs], [Rcarry])
                    else:
                        ubm = ubm2[i % 2]; Rubm = Rubm2[i % 2]
                        P.copy("pool", ubm[:, 0:2], carry[:, l, i, m, :], [Rcarry], [Rubm])
                        P.tt("dve", ubm[:, 2:514], tmpf[tf][:, 0:n], ps[base + 2][:, 0:n], ALU.mult,
                             [Rtmpf[tf], Rps[base + 2]], [Rubm])
                        P.copy("pool", ulast[:, i, 0:2], ubm[:, 512:514], [Rubm], [Rulast])
                        P.ts("dve", ct[:, 0:n], ubm[:, 0:n], w0, None, ALU.mult, None, [Rubm, Rpar], [Rct])
                        P.stt("dve", ct[:, 0:n], ubm[:, 1:n + 1], w1, ct[:, 0:n], ALU.mult, ALU.add, [Rubm, Rpar, Rct], [Rct])
                        P.stt("dve", ct[:, 0:n], ubm[:, 2:n + 2], w2, ct[:, 0:n], ALU.mult, ALU.add, [Rubm, Rpar, Rct], [Rct])
                        P.tt("dve", gT[:, i, 0:n], ct[:, 0:n], ps[base][:, 0:n], ALU.mult, [Rct, Rps[base]], [RgT])
                if i + 2 < NCH:
                    load_in(i + 2)
            for s in range(4):
                b = s % 2
                for dc in range(2):
                    dm = 2 * s + dc
                    for (n0, n) in chunks:
                        yb = 6 + k % 2; k += 1
                        for c in range(NCH):
                            P.mm(ps[yb][:, 0:n], wo[b][:, c, dc * 128:(dc + 1) * 128], gT[:, c, n0:n0 + n], c == 0, c == NCH - 1,
                                 [Rwo[b], RgT], [Rps[yb]])
                        P.tt("dve", r[:, dm, n0:n0 + n], ps[yb][:, 0:n], xa[:, dm, n0:n0 + n], ALU.add, [Rps[yb], Rxa[dm]], [Rr[dm]])
                        ln_prep(dm, n0, n)
                if s + 2 < 4:
                    load_o(s + 2)
            if m == 3:
                store_tok(ulast, 0, 2, conv_out[l, 0:2, :], Rulast)
            if m == 0:
                store_tok(ubs, A_COLS[0] + 14, 2, conv_out[l, 2:4, :], Rubs)
                store_tok(ubs, A_COLS[1] + 14, 2, conv_out[l, 4:6, :], Rubs)

        RkTin = [Res(f"kTin{m}") for m in range(4)]
        Rvin = [Res(f"vin{m}") for m in range(4)]

        def kvproj(m, chunks, N):
            aps, res = arena_switch([("wk0", 2048), ("wk1", 2048), ("wt0", 4096), ("wt1", 4096), ("ktl", NCH * NMAX),
                                     ("vb0", 512), ("vb1", 512)])
            wk = [aps["wk0"].rearrange("p (c f) -> p c f", c=NCH), aps["wk1"].rearrange("p (c f) -> p c f", c=NCH)]
            wt = [aps["wt0"].rearrange("p (c f) -> p c f", c=NCH), aps["wt1"].rearrange("p (c f) -> p c f", c=NCH)]
            ktl = aps["ktl"].rearrange("p (c n) -> p c n", c=NCH)
            vb = [aps["vb0"], aps["vb1"]]
            Rwk = [res["wk0"], res["wk1"]]; Rwt = [res["wt0"], res["wt1"]]; Rktl = res["ktl"]; Rvb = [res["vb0"], res["vb1"]]
            WKV = w_kv.rearrange("(c p) d -> p c d", p=128)
            P.dma("pool", wk[0], WKV[:, :, 0:256], [], [Rwk[0]])
            P.dma("pool", wk[1], WKV[:, :, 256:512], [], [Rwk[1]])
            P.dma("pool", wt[0], WKV[:, :, 0:512], [], [Rwt[0]])
            P.dma("pool", wt[1], WKV[:, :, 512:1024], [], [Rwt[1]])
            yield
            k = 0
            for s in range(4):
                b = s % 2
                for dc in range(2):
                    dm = 2 * s + dc
                    for (n0, n) in chunks:
                        yb = 4 + k % 2; k += 1
                        for c in range(NCH):
                            P.mm(ps[yb][:, 0:n], wk[b][:, c, dc * 128:(dc + 1) * 128], xb[:, c, n0:n0 + n], c == 0, c == NCH - 1,
                                 [Rwk[b], Rxb], [Rps[yb]])
                        P.act(ktl[:, dm, n0:n0 + n], ps[yb][:, 0:n], AF.Copy, [Rps[yb]], [Rktl])
                if s + 2 < 4:
                    P.dma("pool", wk[b], WKV[:, :, (s + 2) * 256:(s + 3) * 256], [], [Rwk[b]])
            P.dma("sp", kT_in[m].rearrange("(c p) t -> p c t", p=128), ktl[:, :, 0:MT], [Rktl], [RkTin[m]])
            if m == 0:
                P.copy("pool", kTs[:, :, :], ktl[:, :, MT:MT + SM], [Rktl], [RkTs])
            blocks = [(tb * 128, 128) for tb in range(4)] + ([(MT, SM)] if m == 0 else [])
            for g in range(4):
                b = g % 2
                outd = k_out if g < 2 else v_out
                cs = (g % 2) * 512
                for (t0, T) in blocks:
                    bank = k % 2; k += 1
                    for c in range(NCH):
                        P.mm(ps[bank][0:T, :], xb[:, c, t0:t0 + T], wt[b][:, c, :], c == 0, c == NCH - 1, [Rxb, Rwt[b]], [Rps[bank]])
                    tf = k % 2
                    P.act(tmpf[tf][0:T, :], ps[bank][0:T, :], AF.Copy, [Rps[bank]], [Rtmpf[tf]])
                    row0 = (MT * m + t0) if t0 < MT else (2048 + t0 - MT)
                    P.dma("sp", outd[row0:row0 + T, cs:cs + 512], tmpf[tf][0:T, :], [Rtmpf[tf]], [Res()])
                    if g >= 2:
                        if t0 < MT:
                            P.copy("dve", vb[tf][0:T, :], ps[bank][0:T, :], [Rps[bank]], [Rvb[tf]])
                            P.dma("sp", v_in[m][t0:t0 + T, cs:cs + 512], vb[tf][0:T, :], [Rvb[tf]], [Rvin[m]])
                        else:
                            P.copy("dve", vs[0:T, cs:cs + 512], ps[bank][0:T, :], [Rps[bank]], [Rvs])
                if g + 2 < 4:
                    P.dma("pool", wt[b], WKV[:, :, (g + 2) * 512:(g + 3) * 512], [], [Rwt[b]])

        RkTall = [Res(f"kTall{m_}") for m_ in range(4)]; Rvall = [Res(f"vall{m_}") for m_ in range(4)]

        def attn(l, m, chunks, N):
            li = l - 2
            aps, res = arena_switch([("qT", NCH * NMAX), ("oT", NCH * NMAX), ("kT0", 8192), ("kT1", 8192), ("vv0", 8192), ("vv1", 8192),
                                     ("e00", 512), ("e01", 512), ("e10", 512), ("e11", 512),
                                     ("sp00", 512), ("sp01", 512), ("sp10", 512), ("sp11", 512),
                                     ("ex0", 512), ("ex1", 512), ("ww0", 512), ("ww1", 512), ("wq0", 2048), ("wq1", 2048)])
            qT = aps["qT"].rearrange("p (c n) -> p c n", c=NCH); RqT = res["qT"]
            oT = aps["oT"].rearrange("p (c n) -> p c n", c=NCH); RoT = res["oT"]
            kTb = [aps["kT0"], aps["kT1"]]; RkTb = [res["kT0"], res["kT1"]]
            vvb = [aps["vv0"].rearrange("p (b d) -> p b d", d=128), aps["vv1"].rearrange("p (b d) -> p b d", d=128)]
            Rvvb = [res["vv0"], res["vv1"]]
            eb = [[aps["e00"], aps["e01"]], [aps["e10"], aps["e11"]]]
            Reb = [[res["e00"], res["e01"]], [res["e10"], res["e11"]]]
            spb = [[aps["sp00"], aps["sp01"]], [aps["sp10"], aps["sp11"]]]
            Rspb = [[res["sp00"], res["sp01"]], [res["sp10"], res["sp11"]]]
            exb = [aps["ex0"], aps["ex1"]]; Rexb = [res["ex0"], res["ex1"]]
            wwb = [aps["ww0"], aps["ww1"]]; Rwwb = [res["ww0"], res["ww1"]]
            wq = [aps["wq0"].rearrange("p (c f) -> p c f", c=NCH), aps["wq1"].rearrange("p (c f) -> p c f", c=NCH)]
            Rwq = [res["wq0"], res["wq1"]]
            WQ = w_q[li].rearrange("(c p) d -> p c d", p=128)
            WO = w_o[li].rearrange("(c p) d -> p c d", p=128)
            P.dma("pool", wq[0], WQ[:, :, 0:256], [], [Rwq[0]])
            P.dma("pool", wq[1], WQ[:, :, 256:512], [], [Rwq[1]])
            yield
            k = 0
            for s in range(4):
                b = s % 2
                for dc in range(2):
                    dm = 2 * s + dc
                    for (n0, n) in chunks:
                        yb = 6 + k % 2; k += 1
                        for c in range(NCH):
                            P.mm(ps[yb][:, 0:n], wq[b][:, c, dc * 128:(dc + 1) * 128], xb[:, c, n0:n0 + n], c == 0, c == NCH - 1,
                                 [Rwq[b], Rxb], [Rps[yb]])
                        P.act(qT[:, dm, n0:n0 + n], ps[yb][:, 0:n], AF.Identity, [Rps[yb]], [RqT], scale=0.125)
                if s + 2 < 4:
                    P.dma("pool", wq[b], WQ[:, :, (s + 2) * 256:(s + 3) * 256], [], [Rwq[b]])
                else:
                    P.dma("pool", wq[b], WO[:, :, (s - 2) * 256:(s - 1) * 256], [], [Rwq[b]])
            if m == 0:
                P.memset("pool", oT[:, :, MT:MT + SM], 0.0, [RoT])

            def emit_chains(blocks, q_aps, nq):
                nb = len(blocks)

                def zmm(a):
                    blk = blocks[a]; K = blk["K"]; pr = a % 2
                    for hh in range(2):
                        Z = 2 * hh + pr
                        P.mm(ps[Z][0:K, 0:nq], blk["kT"][hh], q_aps[hh], True, True, [blk["Rk"], RqT], [Rps[Z]])

                zmm(0)
                for t in range(nb + 1):
                    b = t - 1
                    if b >= 0:
                        bb = blocks[b]; Kb = bb["K"]; pb = b % 2
                        for hh in range(2):
                            X = 4 + hh
                            P.mm(ps[X][0:Kb, 0:nq], uneg[0:Kb, 0:Kb], spb[hh][pb][0:Kb, 0:nq], b == 0, True,
                                 [Rconst, Rspb[hh][pb]], [Rps[X]], skip_group_check=True)
                    if t + 1 < nb:
                        zmm(t + 1)
                    if t < nb:
                        blk = blocks[t]; K = blk["K"]; pr = t % 2
                        for hh in range(2):
                            Z = 2 * hh + pr
                            P.act(eb[hh][pr][0:K, 0:nq], ps[Z][0:K, 0:nq], AF.Exp, [Rps[Z]], [Reb[hh][pr]])
                        if blk["mask"] is not None:
                            for hh in range(2):
                                P.tt("pool", eb[hh][pr][0:K, 0:nq], eb[hh][pr][0:K, 0:nq], blk["mask"], ALU.mult,
                                     [Reb[hh][pr], Rmask], [Reb[hh][pr]])
                    if b >= 0:
                        for hh in range(2):
                            X = 4 + hh
                            P.act(exb[hh][0:Kb, 0:nq], ps[X][0:Kb, 0:nq], AF.Exp, [Rps[X]], [Rexb[hh]])
                    if t < nb:
                        for hh in range(2):
                            P.act(spb[hh][pr][0:K, 0:nq], eb[hh][pr][0:K, 0:nq], AF.Ln, [Reb[hh][pr]], [Rspb[hh][pr]], bias=1.0)
                    if b >= 0:
                        if b < nb - 1:
                            for hh in range(2):
                                X = 4 + hh
                                if bb["restart52"]:
                                    P.mm(ps[X][:, 0:nq], nones[0:Kb, :], spb[hh][pb][0:Kb, 0:nq], True, True,
                                         [Rconst, Rspb[hh][pb]], [Rps[X]], skip_group_check=True)
                                else:
                                    P.mm(ps[X][:, 0:nq], ubar[:, :], spb[hh][pb][:, 0:nq], False, True,
                                         [Rconst, Rspb[hh][pb]], [Rps[X]], skip_group_check=True)
                        for hh in range(2):
                            P.tt("dve", wwb[hh][0:Kb, 0:nq], eb[hh][pb][0:Kb, 0:nq], exb[hh][0:Kb, 0:nq], ALU.mult,
                                 [Reb[hh][pb], Rexb[hh]], [Rwwb[hh]])
                        for hh in range(2):
                            O = 6 + hh
                            P.mm(ps[O][:, 0:nq], bb["v"], wwb[hh][0:Kb, 0:nq], b == 0, b == nb - 1,
                                 [bb["Rv"], Rwwb[hh]], [Rps[O]], skip_group_check=True)

            nkt = 4 * m + 4

            def load_kv(c):
                bf = c % 2
                for lt in range(m + 1):
                    for rk in range(4):
                        kt = 4 * lt + rk
                        P.dma("sp", kTb[bf][:, kt * 512:(kt + 1) * 512], kT_all[lt][rk * D + c * 128:rk * D + (c + 1) * 128, :],
                              [RkTall[lt]], [RkTb[bf]])
                        P.dma("sp", vvb[bf][:, 4 * kt:4 * kt + 4, :],
                              v_all[lt][rk * MT:(rk + 1) * MT, c * 128:(c + 1) * 128].rearrange("(kb p) d -> p kb d", p=128),
                              [Rvall[lt]], [Rvvb[bf]])

            load_kv(0)
            for c in range(NCH):
                if c + 1 < NCH:
                    load_kv(c + 1)
                bf = c % 2; kT = kTb[bf]; vv = vvb[bf]
                nblk = 4 * nkt
                blocks = []
                for bi in range(nblk):
                    B = nblk - 1 - bi
                    kt, kb = divmod(B, 4)
                    mask_ap = None
                    if kt >= 4 * m:
                        mi = (kt - 4 * m) * 4 + kb
                        mask_ap = masks[:, mi * 512:(mi + 1) * 512]
                    blocks.append({"K": 128, "kT": [kT[0:64, B * 128:(B + 1) * 128], kT[64:128, B * 128:(B + 1) * 128]],
                                   "v": vv[:, B, :], "mask": mask_ap, "restart52": False, "Rk": RkTb[bf], "Rv": Rvvb[bf]})
                emit_chains(blocks, [qT[0:64, c, 0:MT], qT[64:128, c, 0:MT]], MT)
                for hh in range(2):
                    ph = slice(64 * hh, 64 * hh + 64)
                    P.copy("dve", oT[ph, c, 0:MT], ps[6 + hh][ph, 0:MT], [Rps[6 + hh]], [RoT])

            if m == 0:
                off_k = NCH * NMAX * 2
                kTbig = AR[:, off_k:off_k + 16384].rearrange("p (c k) -> p c k", c=NCH)
                vbig = AR[:, off_k + 16384:off_k + 32768].rearrange("p (b d) -> p b d", d=D)
                Rkbig = [RkTb[0], RkTb[1]]; Rvbig = [Rvvb[0], Rvvb[1]]
                NQ = 256
                for s_ in range(2):
                    q0 = MT + A_COLS[s_]
                    for kb_ in range(16):
                        for hf_ in range(2):
                            P.dma("pool", vbig[:, kb_, hf_ * 512:(hf_ + 1) * 512],
                                  cv_d[s_, kb_ * 128:(kb_ + 1) * 128, hf_ * 512:(hf_ + 1) * 512], [], [Rvbig])
                    for kb in range(16):
                        h2 = kb % 2
                        P.dma("sp", cst[:, h2 * 1024:(h2 + 1) * 1024], ck_d[s_, kb * 128:(kb + 1) * 128, :], [], [Rcst2[h2]])
                        banks = (5, 7)
                        for half in range(2):
                            bank = banks[half]
                            for cc in range(4):
                                c = half * 4 + cc
                                P.tr(ps[bank][:, cc * 128:(cc + 1) * 128], cst[:, h2 * 1024 + c * 128:h2 * 1024 + (c + 1) * 128],
                                     ident[:, :], [Rcst2[h2], Rid], [Rps[bank]])
                            P.copy("dve", kTbig[:, half * 4:half * 4 + 4, kb * 128:(kb + 1) * 128],
                                   ps[bank][:, :].rearrange("p (c k) -> p c k", c=4), [Rps[bank]], [Rkbig])
                    nb = 17
                    e_ = eb[0]; Re_ = Reb[0]; sp_ = spb[0]; Rsp_ = Rspb[0]; ex_ = exb[0]; Rex_ = Rexb[0]; ww_ = wwb[0]; Rww_ = Rwwb[0]

                    def blkK(a):
                        return SM if a == 0 else 128

                    def zmm_s(a):
                        K = blkK(a)
                        for hh in range(2):
                            Z = 2 * hh + a % 2
                            ph = slice(64 * hh, 64 * hh + 64)
                            for c in range(NCH):
                                if a == 0:
                                    kap = kTs[ph, c, 0:SM]; Rk = RkTs
                                else:
                                    B = 16 - a
                                    kap = kTbig[ph, c, B * 128:(B + 1) * 128]; Rk = Rkbig
                                P.mm(ps[Z][0:K, 16 * c:16 * c + 16], kap, qT[ph, c, q0:q0 + 16], True, True, [Rk, RqT], [Rps[Z]])

                    zmm_s(0)
                    for t in range(nb + 1):
                        b = t - 1
                        if b >= 0:
                            Kb = blkK(b); pb = b % 2
                            P.mm(ps[4][0:Kb, 0:NQ], uneg[0:Kb, 0:Kb], sp_[pb][0:Kb, 0:NQ], b == 0, True,
                                 [Rconst, Rsp_[pb]], [Rps[4]], skip_group_check=True)
                        if t + 1 < nb:
                            zmm_s(t + 1)
                        if t < nb:
                            K = blkK(t); pr = t % 2
                            for hh in range(2):
                                P.act(e_[pr][0:K, 128 * hh:128 * hh + 128], ps[2 * hh + pr][0:K, 0:128], AF.Exp,
                                      [Rps[2 * hh + pr]], [Re_[pr]])
                            if t == 0:
                                P.tt("pool", e_[pr][0:K, 0:NQ], e_[pr][0:K, 0:NQ], smask[0:SM, NQ * s_:NQ * (s_ + 1)], ALU.mult,
                                     [Re_[pr], Rmask], [Re_[pr]])
                        if b >= 0:
                            P.act(ex_[0:Kb, 0:NQ], ps[4][0:Kb, 0:NQ], AF.Exp, [Rps[4]], [Rex_])
                        if t < nb:
                            P.act(sp_[pr][0:K, 0:NQ], e_[pr][0:K, 0:NQ], AF.Ln, [Re_[pr]], [Rsp_[pr]], bias=1.0)
                        if b >= 0:
                            if b < nb - 1:
                                if b == 0:
                                    P.mm(ps[4][:, 0:NQ], nones[0:Kb, :], sp_[pb][0:Kb, 0:NQ], True, True,
                                         [Rconst, Rsp_[pb]], [Rps[4]], skip_group_check=True)
                                else:
                                    P.mm(ps[4][:, 0:NQ], ubar[:, :], sp_[pb][:, 0:NQ], False, True,
                                         [Rconst, Rsp_[pb]], [Rps[4]], skip_group_check=True)
                            P.tt("dve", ww_[0:Kb, 0:NQ].rearrange("p (c h q) -> p c h q", c=NCH, h=2),
                                 e_[pb][0:Kb, 0:NQ].rearrange("p (h c q) -> p c h q", h=2, c=NCH),
                                 ex_[0:Kb, 0:NQ].rearrange("p (h c q) -> p c h q", h=2, c=NCH), ALU.mult, [Re_[pb], Rex_], [Rww_])
                            for c in range(NCH):
                                if b == 0:
                                    vap = vs[0:SM, c * 128:(c + 1) * 128]; Rv = Rvs
                                else:
                                    B = 16 - b
                                    vap = vbig[:, B, c * 128:(c + 1) * 128]; Rv = Rvbig
                                P.mm(ps[6][:, 32 * c:32 * c + 32], vap, ww_[0:Kb, 32 * c:32 * c + 32], b == 0 and c == 0, b == nb - 1,
                                     [Rv, Rww_], [Rps[6]], skip_group_check=True)
                    Oview = ps[6][:, 0:NQ].rearrange("p (c h q) -> p c h q", c=NCH, h=2)
                    P.copy("dve", oT[0:64, :, q0:q0 + 16], Oview[0:64, :, 0, :], [Rps[6]], [RoT])
                    P.copy("dve", oT[64:128, :, q0:q0 + 16], Oview[64:128, :, 1, :], [Rps[6]], [RoT])

            for s in range(4):
                b = s % 2
                for dc in range(2):
                    dm = 2 * s + dc
                    for (n0, n) in chunks:
                        yb = 6 + k % 2; k += 1
                        for c in range(NCH):
                            P.mm(ps[yb][:, 0:n], wq[b][:, c, dc * 128:(dc + 1) * 128], oT[:, c, n0:n0 + n], c == 0, c == NCH - 1,
                                 [Rwq[b], RoT], [Rps[yb]])
                        P.tt("dve", r[:, dm, n0:n0 + n], ps[yb][:, 0:n], xa[:, dm, n0:n0 + n], ALU.add, [Rps[yb], Rxa[dm]], [Rr[dm]])
                        ln_prep(dm, n0, n)
                if s + 2 < 4:
                    P.dma("pool", wq[b], WO[:, :, (s + 2) * 256:(s + 3) * 256], [], [Rwq[b]])

        xa_d3 = xa_d.rearrange("p (c n) -> p c n", c=NCH)
        xb_d3 = xb_d.rearrange("p (c n) -> p c n", c=NCH)
        Rxad = [Res(f"xad{m}") for m in range(4)]
        Rxbd = [Res(f"xbd{m}") for m in range(4)]

        def pass_chunks(m):
            if m == 0:
                return [(0, MT), (MT, SM)], MT + SM
            return [(0, MT)], MT

        steps = []

        def PH(f):
            steps.append(("ph", f))

        def OT(f):
            steps.append(("ot", f))

        def load_x(m):
            for tb in range(4):
                load_tok(xin[MT * m + tb * 128:MT * m + (tb + 1) * 128, :], 128,
                         [(xa, Rxa, "act", ALPHA), (xb, Rxb, "dve", None)], tb * 128, None)
            if m == 0:
                load_tok(xin[2048:2048 + SM, :], SM, [(xa, Rxa, "act", ALPHA), (xb, Rxb, "dve", None)], MT, None)

        def bounce_out(m):
            P.dma("sp", xa_d3[:, :, MT * m:MT * (m + 1)], xa[:, :, 0:MT], [Rxa], [Rxad[m]])
            P.dma("sp", xb_d3[:, :, MT * m:MT * (m + 1)], xb[:, :, 0:MT], [Rxb], [Rxbd[m]])
            if m == 0:
                P.dma("sp", xa_d3[:, :, 2048:2048 + SM], xa[:, :, MT:MT + SM], [Rxa], [Rxad[m]])
                P.dma("sp", xb_d3[:, :, 2048:2048 + SM], xb[:, :, MT:MT + SM], [Rxb], [Rxbd[m]])

        def bounce_in(m):
            P.dma("sp", xa[:, :, 0:MT], xa_d3[:, :, MT * m:MT * (m + 1)], [Rxad[m]], [Rxa])
            P.dma("sp", xb[:, :, 0:MT], xb_d3[:, :, MT * m:MT * (m + 1)], [Rxbd[m]], [Rxb])
            if m == 0:
                P.dma("sp", xa[:, :, MT:MT + SM], xa_d3[:, :, 2048:2048 + SM], [Rxad[m]], [Rxa])
                P.dma("sp", xb[:, :, MT:MT + SM], xb_d3[:, :, 2048:2048 + SM], [Rxbd[m]], [Rxb])

        def store_y(m):
            for tb in range(4):
                store_tok(xa, tb * 128, 128, y_out[MT * m + tb * 128:MT * m + (tb + 1) * 128, :], Rxa)
            if m == 0:
                store_tok(xa, MT, SM, y_out[2048:2048 + SM, :], Rxa)

        def collectives(ms):
            groups = [[0, 1, 2, 3], [4, 5, 6, 7]]
            for m_ in ms:
                P.op("pool", lambda e, m_=m_: e.collective_compute("AllGather", ALU.bypass, replica_groups=groups,
                                                                   ins=[kT_in[m_][:, :]], outs=[kT_all[m_][:, :]]),
                     [RkTin[m_]], [RkTall[m_]], dma=True, cc=True)
                P.op("pool", lambda e, m_=m_: e.collective_compute("AllGather", ALU.bypass, replica_groups=groups,
                                                                   ins=[v_in[m_][:, :]], outs=[v_all[m_][:, :]]),
                     [Rvin[m_]], [Rvall[m_]], dma=True, cc=True)

        for m in range(n_pass):
            chunks, N = pass_chunks(m)
            OT(lambda m=m: load_x(m))
            for l in range(2):
                PH(lambda l=l, ch=chunks, N=N: ffn(l, 0, ch, N))
                OT(lambda l=l, ch=chunks, N=N: layernorm(3 * l + 0, ch, N))
                PH(lambda l=l, m=m, ch=chunks, N=N: conv(l, m, ch, N))
                OT(lambda l=l, ch=chunks, N=N: layernorm(3 * l + 1, ch, N))
                PH(lambda l=l, ch=chunks, N=N: ffn(l, 1, ch, N))
                OT(lambda l=l, ch=chunks, N=N: layernorm(3 * l + 2, ch, N))
            OT(lambda m=m: bounce_out(m))
            PH(lambda m=m, ch=chunks, N=N: kvproj(m, ch, N))
            if do_phase2:
                OT(lambda m=m: collectives([m]))
        if do_phase2:
            for m in range(n_pass):
                chunks, N = pass_chunks(m)
                OT(lambda m=m: bounce_in(m))
                for l in range(2, 4):
                    PH(lambda l=l, ch=chunks, N=N: ffn(l, 0, ch, N))
                    OT(lambda l=l, ch=chunks, N=N: layernorm(3 * l + 0, ch, N))
                    PH(lambda l=l, m=m, ch=chunks, N=N: attn(l, m, ch, N))
                    OT(lambda l=l, ch=chunks, N=N: layernorm(3 * l + 1, ch, N))
                    PH(lambda l=l, ch=chunks, N=N: ffn(l, 1, ch, N))
                    OT(lambda l=l, ch=chunks, N=N: layernorm(3 * l + 2, ch, N, last=(l == 3)))
                OT(lambda m=m: store_y(m))

        gens = {}
        ph_idx = [i_ for i_, st_ in enumerate(steps) if st_[0] == "ph"]

        def begin(i_):
            g_ = steps[i_][1]()
            next(g_)
            gens[i_] = g_

        if ph_idx:
            begin(ph_idx[0])
        for i_, (kind_, f_) in enumerate(steps):
            if kind_ == "ot":
                f_()
            else:
                for _ in gens.pop(i_):
                    pass
                nxt_ = [j_ for j_ in ph_idx if j_ > i_]
                if nxt_:
                    begin(nxt_[0])
        P.emit()
    return nc


_NC_CACHE = {}


def _host_inputs(inp):
    f = np.float32
    xp = np.asarray(inp["x_prompt"], f); xs = np.asarray(inp["x_sample"], f)
    ck = np.asarray(inp["cache_k"], f).reshape(16, 2048, D); cv = np.asarray(inp["cache_v"], f).reshape(16, 2048, D)
    sc = np.asarray(inp["state_conv"], f)
    lng = np.asarray(inp["ln_g"], f).reshape(12, D); lnb = np.asarray(inp["ln_b"], f).reshape(12, D)
    wc = np.asarray(inp["w_conv"], f).reshape(6, D)
    shared = {k: np.ascontiguousarray(np.asarray(inp[k], f)) for k in
              ("w_ffn_gate", "w_ffn_up", "w_ffn_down", "w_conv_in", "w_conv_out", "w_kv", "w_q", "w_o")}
    smask = np.zeros((SM, 512), f)
    for s_ in range(2):
        for h in range(16):
            for t in range(16):
                for kk in range(t):
                    smask[A_COLS[s_] + kk, 256 * s_ + 16 * h + t] = 1.0
    maps = []
    sidx = np.arange(128)[:, None]; tidx = np.arange(512)[None, :]
    for c in range(8):
        b, j = divmod(c, 4)
        xin = np.zeros((NTOK, D), f)
        for m in range(4):
            t0 = 512 * (4 * m + j)
            xin[512 * m:512 * (m + 1)] = xp[b, t0:t0 + 512]
            if t0 > 0:
                xin[2048 + 4 * m:2048 + 4 * m + 4] = xp[b, t0 - 4:t0]
        for s_ in range(2):
            xin[2048 + A_COLS[s_]:2048 + A_COLS[s_] + 16] = xs[2 * c + s_]
        prm = np.concatenate([lng, lnb, wc, sc[:, 2 * c:2 * c + 2].reshape(8, D)], 0)
        masks = np.zeros((128, 16 * 512), f)
        for i in range(4):
            for kb in range(4):
                mi = i * 4 + kb
                masks[:, mi * 512:(mi + 1) * 512] = ((512 * i + 128 * kb + sidx) < (512 * j + tidx)).astype(f)
        hm = np.full((128, 1), 0.0 if j == 0 else 1.0, f)
        d = {"xin": xin, "prm": np.ascontiguousarray(prm), "masks": masks, "smask": smask, "hmask": hm,
             "cache_k": np.ascontiguousarray(ck[2 * c:2 * c + 2]), "cache_v": np.ascontiguousarray(cv[2 * c:2 * c + 2])}
        d.update(shared)
        maps.append(d)
    return maps


def _assemble(results):
    f = np.float32
    y_p = np.zeros((2, 8192, D), f); k_p = np.zeros((2, 8192, D), f); v_p = np.zeros((2, 8192, D), f)
    y_s = np.zeros((16, 16, D), f); k_s = np.zeros((16, 16, D), f); v_s = np.zeros((16, 16, D), f)
    conv_p = np.zeros((2, 2, 2, D), f); conv_s = np.zeros((2, 16, 2, D), f)
    for c in range(8):
        b, j = divmod(c, 4)
        rr = results[c]
        for m in range(4):
            t0 = 512 * (4 * m + j)
            y_p[b, t0:t0 + 512] = rr["y_out"][512 * m:512 * (m + 1)]
            k_p[b, t0:t0 + 512] = rr["k_out"][512 * m:512 * (m + 1)]
            v_p[b, t0:t0 + 512] = rr["v_out"][512 * m:512 * (m + 1)]
        for s_ in range(2):
            a0 = 2048 + A_COLS[s_]
            y_s[2 * c + s_] = rr["y_out"][a0:a0 + 16]
            k_s[2 * c + s_] = rr["k_out"][a0:a0 + 16]
            v_s[2 * c + s_] = rr["v_out"][a0:a0 + 16]
            conv_s[:, 2 * c + s_] = rr["conv_out"][:, 2 + 2 * s_:4 + 2 * s_]
        if j == 3:
            conv_p[:, b] = rr["conv_out"][:, 0:2]
    return (y_p, y_s, k_p.reshape(2, 8192, 16, 64), v_p.reshape(2, 8192, 16, 64), conv_p,
            k_s.reshape(16, 16, 16, 64), v_s.reshape(16, 16, 16, 64), conv_s)


def kernel(**inputs):
    if "nc" not in _NC_CACHE:
        _NC_CACHE["nc"] = build()
    nc = _NC_CACHE["nc"]
    maps = _host_inputs(inputs)
    res = run_bass_kernel_spmd(nc, maps, core_ids=list(range(8)))
    return _assemble(res.results)
```
